# Optimizing a Trainium2 kernel written in Bass

```python
import math
import jax, jax.numpy as jnp
from jax import lax
import numpy as np

D_MODEL = 1024
BATCH = 32
SEQ = 2048
DEPTH = 2

CTX_LEN = 256
GRID_W = 64
N_EVEN = (DEPTH + 1) // 2
N_ODD = DEPTH // 2

A_GROUPS = 4
A_GROUP_DIM = 128
A_WIDTH = A_GROUPS * A_GROUP_DIM
A_CHUNK = 128
B_HEADS = 4
B_HEAD_DIM = 128
B_WIDTH = B_HEADS * B_HEAD_DIM
B_CONV = 3
GDN_CHUNK = 64
AB_SPLITS = (A_WIDTH, 2 * A_WIDTH, 2 * A_WIDTH + 3 * B_WIDTH, 2 * A_WIDTH + 4 * B_WIDTH,
             2 * A_WIDTH + 4 * B_WIDTH + 2 * B_HEADS)
AB_IN = 2 * A_WIDTH + 4 * B_WIDTH + 4 * B_HEADS
AB_MIX = A_WIDTH + B_WIDTH
C_HEADS = 8
C_NOPE = 128
C_ROPE = 64
C_VDIM = 128
C_Q_LORA = 384
C_KV_LORA = 256
C_IN = C_Q_LORA + C_KV_LORA + C_ROPE
Q_BLOCK = 128
ROPE_THETA = 10000.0
D_FF = 2816
FFN_CONV = 3
EPS = 1e-6

kernel_name = 'hybrid_gmlp_gdn_mla_convffn_prefix'


def rms_norm(x, g):
    xf = x.astype(jnp.float32)
    y = xf * lax.rsqrt(jnp.mean(xf * xf, axis=-1, keepdims=True) + EPS)
    return (y * g.astype(jnp.float32)).astype(x.dtype)


def layer_norm(x, g, b):
    xf = x.astype(jnp.float32)
    mu = jnp.mean(xf, axis=-1, keepdims=True)
    var = jnp.mean(jnp.square(xf - mu), axis=-1, keepdims=True)
    y = (xf - mu) * lax.rsqrt(var + EPS) * g.astype(jnp.float32) + b.astype(jnp.float32)
    return y.astype(x.dtype)


def modulate(h, shift, scale):
    return h * (1 + scale) + shift


def ada_mod(cond, w, b):
    m = (jax.nn.silu(cond) @ w + b)[..., None, :]
    return jnp.split(m, 6, axis=-1)


def dwconv(x, w):
    k = w.shape[0]
    return lax.conv_general_dilated(x, w[:, None, :], (1,), [(k // 2, k // 2)],
                                    dimension_numbers=('NWC', 'WIO', 'NWC'),
                                    feature_group_count=x.shape[-1])


def axial_rope(n, rot_dim):
    rows = n // GRID_W
    row = jnp.repeat(jnp.arange(rows, dtype=jnp.float32), GRID_W)
    col = jnp.tile(jnp.arange(GRID_W, dtype=jnp.float32), rows)
    n_freq = rot_dim // 4
    inv = ROPE_THETA ** (-jnp.arange(n_freq, dtype=jnp.float32) / n_freq)
    ang = jnp.concatenate([row[:, None] * inv, col[:, None] * inv], axis=-1)
    return jnp.cos(ang), jnp.sin(ang)


def apply_rope(x, cos, sin):
    h = x.shape[-1] // 2
    x1, x2 = x[..., :h], x[..., h:]
    cos = cos.astype(x.dtype)
    sin = sin.astype(x.dtype)
    return jnp.concatenate([x1 * cos - x2 * sin, x2 * cos + x1 * sin], axis=-1)


def chunk_gmlp(u, v, ln_g, ln_b, ws, bs):
    u = jax.nn.gelu(u, approximate=False)
    v = layer_norm(jax.nn.gelu(v, approximate=False), ln_g, ln_b)
    bn, l, _ = v.shape
    vc = v.reshape(bn, l // A_CHUNK, A_CHUNK, A_GROUPS, A_GROUP_DIM)
    s = jnp.einsum('gij,bnjgd->bnigd', ws, vc) + bs.T[None, None, :, :, None]
    return u * s.reshape(bn, l, A_WIDTH)


def gdn_chunked(q, k, v, g, beta, s0):
    bn, l, h, dk = q.shape
    cs = GDN_CHUNK
    n = l // cs
    to_c = lambda t: t.reshape(bn, n, cs, h, t.shape[-1]).transpose(1, 0, 3, 2, 4)
    qc, kc, vc = to_c(q), to_c(k), to_c(v)
    gc = g.reshape(bn, n, cs, h).transpose(1, 0, 3, 2)
    bc = beta.reshape(bn, n, cs, h).transpose(1, 0, 3, 2)
    gcum = jnp.cumsum(gc, axis=-1)
    tri_incl = jnp.tril(jnp.ones((cs, cs), bool))
    tri_strict = jnp.tril(jnp.ones((cs, cs), bool), -1)
    diff = gcum[..., :, None] - gcum[..., None, :]
    decay = jnp.exp(jnp.where(tri_incl, diff, -jnp.inf))
    kb = kc * bc[..., None]
    vb = vc * bc[..., None]
    a_mat = jnp.where(tri_strict, jnp.einsum('nbhid,nbhjd->nbhij', kb, kc) * decay, 0.0)
    eye = jnp.eye(cs, dtype=jnp.float32)
    t_mat = lax.linalg.triangular_solve(eye + a_mat, jnp.broadcast_to(eye, a_mat.shape),
                                        left_side=True, lower=True, unit_diagonal=True)
    u = jnp.einsum('nbhij,nbhjd->nbhid', t_mat, vb)
    w = jnp.einsum('nbhij,nbhjd->nbhid', t_mat, kb * jnp.exp(gcum)[..., None])
    attn = jnp.einsum('nbhid,nbhjd->nbhij', qc, kc) * decay
    qg = qc * jnp.exp(gcum)[..., None]
    kdec = kc * jnp.exp(gcum[..., -1:] - gcum)[..., None]
    gend = jnp.exp(gcum[..., -1])

    def step(s, xs):
        u_i, w_i, qg_i, kd_i, at_i, ge_i = xs
        v_new = u_i - jnp.einsum('bhck,bhkv->bhcv', w_i, s)
        o_i = jnp.einsum('bhck,bhkv->bhcv', qg_i, s) + jnp.einsum('bhij,bhjv->bhiv', at_i, v_new)
        s = s * ge_i[..., None, None] + jnp.einsum('bhck,bhcv->bhkv', kd_i, v_new)
        return s, o_i

    s_fin, o = lax.scan(step, s0, (u, w, qg, kdec, attn, gend))
    o = o.transpose(1, 0, 3, 2, 4).reshape(bn, l, h, v.shape[-1])
    return o, s_fin


def gated_deltanet(qkv, gate, alpha_raw, beta_raw, conv_w, a_log, dt_bias, norm_g, s0_pair):
    bn, l, _ = qkv.shape
    hq = jax.nn.silu(dwconv(qkv, conv_w)).astype(jnp.float32)
    q, k, v = jnp.split(hq, 3, axis=-1)
    l2 = lambda t: t * lax.rsqrt(jnp.sum(t * t, axis=-1, keepdims=True) + EPS)
    q = l2(q.reshape(bn, l, B_HEADS, B_HEAD_DIM)) * (B_HEAD_DIM ** -0.5)
    k = l2(k.reshape(bn, l, B_HEADS, B_HEAD_DIM))
    v = v.reshape(bn, l, B_HEADS, B_HEAD_DIM)
    a = alpha_raw.astype(jnp.float32).reshape(bn, l, 2, B_HEADS)
    beta = jax.nn.sigmoid(beta_raw.astype(jnp.float32).reshape(bn, l, 2, B_HEADS))
    log_alpha = -jnp.exp(a_log.astype(jnp.float32)) * jax.nn.softplus(a + dt_bias.astype(jnp.float32))
    o_f, s_f = gdn_chunked(q, k, v, log_alpha[:, :, 0], beta[:, :, 0], s0_pair[0])
    flip = lambda t: jnp.flip(t, axis=1)
    o_b, s_b = gdn_chunked(flip(q), flip(k), flip(v), flip(log_alpha[:, :, 1]), flip(beta[:, :, 1]), s0_pair[1])
    o = o_f + flip(o_b)
    o = rms_norm(o, norm_g) * jax.nn.silu(gate.astype(jnp.float32).reshape(bn, l, B_HEADS, B_HEAD_DIM))
    return o.reshape(bn, l, B_WIDTH).astype(qkv.dtype), (s_f, s_b)


def ab_mixer(h_ctx, h_lat, w_in, a_ln_g, a_ln_b, a_ws, a_bs, conv_w, a_log, dt_bias, norm_g, w_out,
             need_ctx_out):
    def run(h, state0, want_out):
        a_u, a_v, qkv, gate, alpha_raw, beta_raw = jnp.split(h @ w_in, AB_SPLITS, axis=-1)
        y_b, state = gated_deltanet(qkv, gate, alpha_raw, beta_raw, conv_w, a_log, dt_bias, norm_g, state0)
        if not want_out:
            return None, state
        y_a = chunk_gmlp(a_u, a_v, a_ln_g, a_ln_b, a_ws, a_bs)
        return jnp.concatenate([y_a, y_b], axis=-1) @ w_out, state

    zero = jnp.zeros((h_ctx.shape[0], B_HEADS, B_HEAD_DIM, B_HEAD_DIM), jnp.float32)
    y_ctx, ctx_state = run(h_ctx, (zero, zero), need_ctx_out)
    y_lat, _ = run(h_lat, ctx_state, True)
    return y_ctx, y_lat


def block_attention(qn, qr, kn, kr, v):
    bn, l, h, _ = qn.shape
    nb = l // Q_BLOCK
    scale = (C_NOPE + C_ROPE) ** -0.5
    blocks = lambda t: t.reshape(bn, nb, Q_BLOCK, h, t.shape[-1]).transpose(1, 0, 2, 3, 4)

    def one(blk):
        qn_b, qr_b = blk
        s = jnp.einsum('bqhd,bkhd->bhqk', qn_b, kn) + jnp.einsum('bqhd,bkd->bhqk', qr_b, kr)
        p = jax.nn.softmax(s.astype(jnp.float32) * scale, axis=-1).astype(v.dtype)
        return jnp.einsum('bhqk,bkhd->bqhd', p, v)

    o = lax.map(one, (blocks(qn), blocks(qr)))
    return o.transpose(1, 0, 2, 3, 4).reshape(bn, l, h * v.shape[-1])


def mla_mixer(h_ctx, h_lat, w_in, q_norm_g, kv_norm_g, w_uq, w_ukv, w_out, need_ctx_out):
    n_lat = h_lat.shape[1]
    cos, sin = axial_rope(n_lat, C_ROPE)
    rot_q = lambda t: apply_rope(t, cos[:, None, :], sin[:, None, :])
    rot_k = lambda t: apply_rope(t, cos, sin)
    no_rot = lambda t: t

    def queries(cq, rot):
        bn, l, _ = cq.shape
        q = (rms_norm(cq, q_norm_g) @ w_uq).reshape(bn, l, C_HEADS, C_NOPE + C_ROPE)
        return q[..., :C_NOPE], rot(q[..., C_NOPE:])

    def keys(kv_part, rot):
        bn, l, _ = kv_part.shape
        ckv, kr = jnp.split(kv_part, [C_KV_LORA], axis=-1)
        kv = (rms_norm(ckv, kv_norm_g) @ w_ukv).reshape(bn, l, C_HEADS, C_NOPE + C_VDIM)
        return kv[..., :C_NOPE], rot(kr), kv[..., C_NOPE:]

    z_lat = h_lat @ w_in
    kn_l, kr_l, v_l = keys(z_lat[..., C_Q_LORA:], rot_k)
    qn_l, qr_l = queries(z_lat[..., :C_Q_LORA], rot_q)
    if need_ctx_out:
        z_ctx = h_ctx @ w_in
        kv_ctx = z_ctx[..., C_Q_LORA:]
    else:
        kv_ctx = h_ctx @ w_in[:, C_Q_LORA:]
    kn_c, kr_c, v_c = keys(kv_ctx, no_rot)
    kn = jnp.concatenate([kn_c, kn_l], axis=1)
    kr = jnp.concatenate([kr_c, kr_l], axis=1)
    v = jnp.concatenate([v_c, v_l], axis=1)
    y_lat = block_attention(qn_l, qr_l, kn, kr, v) @ w_out
    y_ctx = None
    if need_ctx_out:
        qn_c, qr_c = queries(z_ctx[..., :C_Q_LORA], no_rot)
        y_ctx = block_attention(qn_c, qr_c, kn_c, kr_c, v_c) @ w_out
    return y_ctx, y_lat


def conv_ffn(h, w_up, conv_w, w_down):
    g, u = jnp.split(dwconv(h @ w_up, conv_w), 2, axis=-1)
    return (jax.nn.silu(g) * u) @ w_down


def setup_inputs(seed: int = 0) -> dict:
    key = jax.random.key(seed)
    ks = iter(jax.random.split(key, 32))
    nrm = lambda shape, s: jax.random.normal(next(ks), shape, jnp.float32) * s
    gain = lambda shape: 1.0 + nrm(shape, 0.02)
    d = D_MODEL
    a_log = jnp.log(jax.random.uniform(next(ks), (N_EVEN, 2, B_HEADS), jnp.float32, 1.0, 16.0))
    dt = jnp.exp(jax.random.uniform(next(ks), (N_EVEN, 2, B_HEADS), jnp.float32,
                                    math.log(1e-3), math.log(1e-1)))
    dt_bias = dt + jnp.log(-jnp.expm1(-dt))
    return {
        'x': nrm((BATCH, SEQ, d), 1.0),
        'c': nrm((BATCH, d), 1.0),
        'ctx': nrm((BATCH, CTX_LEN, d), 1.0),
        'c_ctx': nrm((d,), 1.0),
        'ada_w': nrm((DEPTH, d, 6 * d), d ** -0.5),
        'ada_b': nrm((DEPTH, 6 * d), 0.01),
        'norm1_g': gain((DEPTH, d)),
        'norm2_g': gain((DEPTH, d)),
        'ab_w_in': nrm((N_EVEN, d, AB_IN), d ** -0.5),
        'a_ln_g': gain((N_EVEN, A_WIDTH)),
        'a_ln_b': nrm((N_EVEN, A_WIDTH), 0.01),
        'a_ws': nrm((N_EVEN, A_GROUPS, A_CHUNK, A_CHUNK), 0.5 * A_CHUNK ** -0.5),
        'a_bs': 1.0 + nrm((N_EVEN, A_GROUPS, A_CHUNK), 0.1),
        'b_conv_w': nrm((N_EVEN, B_CONV, 3 * B_WIDTH), B_CONV ** -0.5),
        'b_a_log': a_log,
        'b_dt_bias': dt_bias,
        'b_norm_g': gain((N_EVEN, B_HEAD_DIM)),
        'ab_w_out': nrm((N_EVEN, AB_MIX, d), AB_MIX ** -0.5),
        'mla_w_in': nrm((N_ODD, d, C_IN), d ** -0.5),
        'mla_q_norm_g': gain((N_ODD, C_Q_LORA)),
        'mla_kv_norm_g': gain((N_ODD, C_KV_LORA)),
        'mla_w_uq': nrm((N_ODD, C_Q_LORA, C_HEADS * (C_NOPE + C_ROPE)), C_Q_LORA ** -0.5),
        'mla_w_ukv': nrm((N_ODD, C_KV_LORA, C_HEADS * (C_NOPE + C_VDIM)), C_KV_LORA ** -0.5),
        'mla_w_out': nrm((N_ODD, C_HEADS * C_VDIM, d), (C_HEADS * C_VDIM) ** -0.5),
        'ffn_w_up': nrm((DEPTH, d, 2 * D_FF), d ** -0.5),
        'ffn_conv_w': nrm((DEPTH, FFN_CONV, 2 * D_FF), FFN_CONV ** -0.5),
        'ffn_w_down': nrm((DEPTH, D_FF, d), D_FF ** -0.5),
        'final_g': gain((d,)),
    }


def reference(x, c, ctx, c_ctx, ada_w, ada_b, norm1_g, norm2_g, ab_w_in, a_ln_g, a_ln_b, a_ws, a_bs,
              b_conv_w, b_a_log, b_dt_bias, b_norm_g, ab_w_out, mla_w_in, mla_q_norm_g, mla_kv_norm_g,
              mla_w_uq, mla_w_ukv, mla_w_out, ffn_w_up, ffn_conv_w, ffn_w_down, final_g):
    lat, cx = x, ctx
    for i in range(DEPTH):
        last = i == DEPTH - 1
        j = i // 2
        sh1, sc1, g1, sh2, sc2, g2 = ada_mod(c, ada_w[i], ada_b[i])
        csh1, csc1, cg1, csh2, csc2, cg2 = ada_mod(c_ctx, ada_w[i], ada_b[i])
        h_lat = modulate(rms_norm(lat, norm1_g[i]), sh1, sc1)
        h_ctx = modulate(rms_norm(cx, norm1_g[i]), csh1, csc1)
        if i % 2 == 0:
            y_ctx, y_lat = ab_mixer(h_ctx, h_lat, ab_w_in[j], a_ln_g[j], a_ln_b[j], a_ws[j], a_bs[j],
                                    b_conv_w[j], b_a_log[j], b_dt_bias[j], b_norm_g[j], ab_w_out[j],
                                    not last)
        else:
            y_ctx, y_lat = mla_mixer(h_ctx, h_lat, mla_w_in[j], mla_q_norm_g[j], mla_kv_norm_g[j],
                                     mla_w_uq[j], mla_w_ukv[j], mla_w_out[j], not last)
        lat = lat + g1 * y_lat
        lat = lat + g2 * conv_ffn(modulate(rms_norm(lat, norm2_g[i]), sh2, sc2),
                                  ffn_w_up[i], ffn_conv_w[i], ffn_w_down[i])
        if not last:
            cx = cx + cg1 * y_ctx
            cx = cx + cg2 * conv_ffn(modulate(rms_norm(cx, norm2_g[i]), csh2, csc2),
                                     ffn_w_up[i], ffn_conv_w[i], ffn_w_down[i])
    return rms_norm(lat, final_g)
```

```python
from contextlib import ExitStack
import numpy as np
import concourse.bass as bass
import concourse.mybir as mybir
from concourse.bass_utils import run_bass_kernel_spmd

F32 = mybir.dt.float32
BF16 = mybir.dt.bfloat16
AF = mybir.ActivationFunctionType
ALU = mybir.AluOpType
AX = mybir.AxisListType

EPOCH = 16000
RING = 12
SAME_ENGINE_RAW = True
INV_BF16 = False
EPS = 1e-6
NCORES = 8
D = 1024
SEQ = 2048
CTX = 256
DFF = 2816


class Res:
    __slots__ = ("name", "last_w", "readers")

    def __init__(self, name):
        self.name = name
        self.last_w = None
        self.readers = {}


class Op:
    __slots__ = ("eng", "fn", "deps", "signal", "is_dma", "sig", "pos", "ring_wait", "d2d")

    def __init__(self, eng, fn, is_dma):
        self.eng = eng
        self.fn = fn
        self.deps = {}
        self.signal = False
        self.is_dma = is_dma
        self.sig = None
        self.ring_wait = None
        self.d2d = False


class Prog:
    STREAMS = ("pe", "act", "dve", "pool", "sp")

    def __init__(self, nc):
        self.nc = nc
        self.ops = {s: [] for s in self.STREAMS}
        self.es = ExitStack()
        self.n = 0
        self.uid = 0

    def sbuf(self, name, shape, dt, es=None):
        self.uid += 1
        return (es or self.es).enter_context(self.nc.sbuf_tensor(f"{name}_{self.uid}", list(shape), dt))

    def psum(self, name, shape, dt=F32):
        return self.es.enter_context(self.nc.psum_tensor(name, list(shape), dt))

    def res(self, name="r"):
        return Res(name)

    def op(self, eng, fn, reads=(), writes=(), dma=False):
        o = Op(eng, fn, dma)
        o.pos = self.n
        self.n += 1
        for r in reads:
            if r.last_w is not None:
                self._dep(o, r.last_w, True)
        for r in writes:
            if r.last_w is not None:
                self._dep(o, r.last_w, False)
            for rd in r.readers.values():
                self._dep(o, rd, False)
        for r in reads:
            r.readers[(o.eng, o.pos) if dma else o.eng] = o
        for r in writes:
            r.last_w = o
            r.readers = {}
        self.ops[eng].append(o)
        return o

    def _dep(self, o, src, raw):
        if src is o:
            return
        if not src.is_dma and not o.is_dma and src.eng == o.eng:
            if o.eng == "pe" or not (raw and SAME_ENGINE_RAW):
                return
        key = (src.eng, src.pos) if src.is_dma else src.eng
        prev = o.deps.get(key)
        if prev is None or prev.pos < src.pos:
            o.deps[key] = src
        src.signal = True

    def barrier(self):
        lasts = []
        for s in self.STREAMS:
            ol = self.ops[s]
            if not ol:
                continue
            for o in reversed(ol):
                if o.fn is not None and not o.is_dma:
                    lasts.append(o)
                    break
            nd = 0
            for o in reversed(ol):
                if o.is_dma and not o.d2d:
                    lasts.append(o)
                    nd += 1
                    if nd >= RING:
                        break
        for s in self.STREAMS:
            o = Op(s, None, False)
            o.pos = self.n
            self.n += 1
            for src in lasts:
                if src.eng == s and not src.is_dma:
                    continue
                key = (src.eng, src.pos) if src.is_dma else src.eng
                o.deps[key] = src
                src.signal = True
            self.ops[s].append(o)

    def dma(self, out, in_, reads=(), writes=(), eng="sp", d2d=False, **kw):
        o = self.op(eng, lambda e: e.dma_start(out=out, in_=in_, **kw), reads, writes, dma=True)
        o.d2d = d2d
        return o

    def mm(self, out, lhsT, rhs, start, stop, reads=(), writes=(), **kw):
        return self.op("pe", lambda e: e.matmul(out, lhsT, rhs, start=start, stop=stop, **kw), reads, writes)

    def act(self, out, in_, func, reads=(), writes=(), **kw):
        return self.op("act", lambda e: e.activation(out=out, in_=in_, func=func, **kw), reads, writes)

    def dve(self, fn, reads=(), writes=()):
        return self.op("dve", fn, reads, writes)

    def stt(self, out, in0, scalar, in1, op0, op1, reads=(), writes=(), eng="dve"):
        return self.op(eng, lambda e: e.scalar_tensor_tensor(out, in0, scalar, in1, op0, op1), reads, writes)

    def ts(self, out, in0, s1, s2, op0, op1=None, reads=(), writes=(), eng="dve"):
        if op1 is None:
            return self.op(eng, lambda e: e.tensor_scalar(out, in0, s1, None, op0), reads, writes)
        return self.op(eng, lambda e: e.tensor_scalar(out, in0, s1, s2, op0, op1), reads, writes)

    def tt(self, out, in0, in1, op, reads=(), writes=(), eng="dve"):
        return self.op(eng, lambda e: e.tensor_tensor(out, in0, in1, op), reads, writes)

    def recip(self, out, in_, reads=(), writes=()):
        return self.op("dve", lambda e: e.reciprocal(out, in_), reads, writes)

    def copy(self, out, in_, reads=(), writes=(), eng="dve"):
        if eng == "act":
            return self.op(eng, lambda e: e.activation(out=out, in_=in_, func=AF.Copy), reads, writes)
        return self.op(eng, lambda e: e.tensor_copy(out, in_), reads, writes)

    def memset(self, ap, v, writes=(), eng="pool"):
        return self.op(eng, lambda e: e.memset(ap, v), (), writes)

    def emit(self, final_wait_ops=()):
        nc = self.nc
        sems = {}

        def get_sem(name):
            if name not in sems:
                sems[name] = self.es.enter_context(nc.semaphore(name))
            return sems[name]

        for s in self.STREAMS:
            cnt = 0
            nd = 0
            hist = []
            for o in self.ops[s]:
                if o.is_dma:
                    slot = nd % RING
                    o.sig = (get_sem(f"d_{s}_{slot}"), 16 * (nd // RING + 1))
                    if nd >= RING:
                        o.ring_wait = hist[nd - RING].sig
                    hist.append(o)
                    nd += 1
                elif o.signal:
                    cnt += 1
                    ep = (cnt - 1) // EPOCH
                    o.sig = (get_sem(f"c_{s}_{ep}"), cnt - ep * EPOCH)
        final_sigs = [o.sig for o in final_wait_ops]
        with nc.Block() as block:
            def make(s):
                def body(e):
                    waited = {}

                    def wait(sig):
                        sem, val = sig
                        k = id(sem)
                        if waited.get(k, 0) >= val:
                            return
                        waited[k] = val
                        e.wait_ge(sem, val)

                    for o in self.ops[s]:
                        if o.ring_wait is not None:
                            wait(o.ring_wait)
                        for d in o.deps.values():
                            wait(d.sig)
                        if o.fn is None:
                            continue
                        ins = o.fn(e)
                        if o.is_dma:
                            ins.then_inc(o.sig[0], 16)
                        elif o.signal:
                            ins.then_inc(o.sig[0], 1)
                    if s == "sp":
                        for sg in final_sigs:
                            wait(sg)
                return body

            block.tensor(make("pe"))
            block.scalar(make("act"))
            block.vector(make("dve"))
            block.gpsimd(make("pool"))
            block.sync(make("sp"))
        self.es.close()


class Builder:
    def __init__(self, nb, stages, dbg=()):
        self.nb = nb
        self.stages = stages
        self.dbg = dbg
        nc = self.nc = bass.Bass("TRN2", target_bir_lowering=False)
        self.P = P = Prog(nc)
        self.inputs = {}
        nj = nb + 1
        self.nj = nj

        def inp(name, shape, dt=F32):
            t = nc.dram_tensor(name, list(shape), dt, kind="ExternalInput").ap()
            self.inputs[name] = t
            return t

        def scratch(name, shape, dt=BF16):
            return nc.dram_tensor(name, list(shape), dt, kind="Internal").ap()

        self.xT = inp("xT", [nb, 128, 8, SEQ])
        self.cT = inp("cT", [nb, 128, 8, CTX])
        self.ccT = inp("ccT", [128, 8, nj])
        self.ada_w = inp("ada_w", [2, D, 6 * D])
        self.ada_bT = inp("ada_bT", [128, 2, 48])
        self.n1g = inp("n1g", [128, 2, 8])
        self.n2g = inp("n2g", [128, 2, 8])
        self.fing = inp("fing", [128, 8])
        self.w_up = inp("ffn_w_up", [2, D, 2 * DFF])
        self.w_down = inp("ffn_w_down", [2, DFF, D])
        self.ffn_cw = inp("ffn_cw", [128, 2, 44, 3])
        self.a_win = inp("a_win", [D, 3088])
        self.a_wout = inp("a_wout", [D, D])
        self.a_lng = inp("a_lng", [128, 512])
        self.a_lnb = inp("a_lnb", [128, 512])
        self.a_wsT = inp("a_wsT", [128, 4, 128])
        self.a_bsb = inp("a_bsb", [128, 4, 128])
        self.b_cw = inp("b_cw", [128, 12, 3])
        self.b_alog = inp("b_alog", [128, 8])
        self.b_dtb = inp("b_dtb", [128, 8])
        self.b_ng = inp("b_ng", [128, 1])
        self.ident = inp("ident", [128, 128])
        self.maskc = inp("maskc", [64, 2, 64])
        self.strict = inp("strict", [64, 2, 64])
        self.WINA = scratch("WINA", [128, 8, 3088])
        self.WOA = scratch("WOA", [128, 8, D])
        self.r_WA = P.res("WA")
        self.QKV = scratch("QKV", [CTX + SEQ, 1552], F32)
        self.GATE = scratch("GATE", [128, 4, CTX + SEQ])
        self.YA = scratch("YA", [128, 4, CTX + SEQ])
        self.r_QKV, self.r_GATE, self.r_YA = P.res(), P.res(), P.res()
        self.m_win = inp("m_win", [D, 768])
        self.m_wuq = inp("m_wuq", [384, 2048])
        self.m_wukv = inp("m_wukv", [256, 2048])
        self.m_wout = inp("m_wout", [D, D])
        self.m_qng = inp("m_qng", [128, 3])
        self.m_kvng = inp("m_kvng", [128, 2])
        self.ropeC = inp("ropeC", [64, SEQ])
        self.ropeS = inp("ropeS", [64, SEQ])
        self.WIN1 = scratch("WIN1", [128, 8, 768])
        self.WUQ = scratch("WUQ", [128, 3, 2048])
        self.WUKV = scratch("WUKV", [128, 2, 2048])
        self.WO1 = scratch("WO1", [128, 8, D])
        self.r_W1 = P.res("W1")
        self.outT = nc.dram_tensor("outT", [nb, 128, 8, SEQ], F32, kind="ExternalOutput").ap()
        self.dbg_out = {}
        self.WU = scratch("WU", [2, 22, 128, 8, 2, 128])
        self.WD = scratch("WD", [2, 128, 22, D])
        self.r_WU = [P.res("WU0"), P.res("WU1")]
        self.r_WD = [P.res("WD0"), P.res("WD1")]
        self.Rl = P.sbuf("Rl", [128, 8, SEQ], F32)
        self.Rc = P.sbuf("Rc", [128, 8, CTX], F32)
        self.r_Rl = P.res("Rl")
        self.r_Rc = P.res("Rc")
        self.MOD = P.sbuf("MOD", [128, 2, 48, nj], F32)
        self.GS = P.sbuf("GS", [128, 2, 2, 8, nj], F32)
        self.r_MOD = P.res("MOD")
        self.ONESb = P.sbuf("ONESb", [128, 128], BF16)
        self.EPSC = P.sbuf("EPSC", [128, 4], F32)
        self.r_const = P.res("const")
        self.PS = P.psum("PS", [128, 4096], F32)
        self.r_ps = [P.res(f"ps{i}") for i in range(8)]
        self.out_dmas = []

    def bank(self, i):
        return self.PS[:, 512 * i:512 * (i + 1)]

    def cast_weights(self):
        P = self.P

        def ffn(l):
            src = self.w_up[l].rearrange("(kc p) (gu c j) -> c p kc gu j", p=128, gu=2, j=128)
            for c in range(22):
                for gu in range(2):
                    P.dma(self.WU[l, c][:, :, gu, :], src[c][:, :, gu, :], writes=[self.r_WU[l]], eng="pool", d2d=True)
            srcd = self.w_down[l].rearrange("(c p) o -> p c o", p=128)
            for c0 in range(0, 22, 2):
                P.dma(self.WD[l][:, c0:c0 + 2, :], srcd[:, c0:c0 + 2, :], writes=[self.r_WD[l]], eng="pool", d2d=True)

        wsrc = self.a_win.rearrange("(kc p) n -> p kc n", p=128)
        for c0 in range(0, 3088, 772):
            P.dma(self.WINA[:, :, c0:c0 + 772], wsrc[:, :, c0:c0 + 772], writes=[self.r_WA], eng="pool", d2d=True)
        P.dma(self.WOA, self.a_wout.rearrange("(kc p) n -> p kc n", p=128), writes=[self.r_WA], eng="pool", d2d=True)
        ffn(0)
        P.dma(self.WIN1, self.m_win.rearrange("(kc p) n -> p kc n", p=128), writes=[self.r_W1], eng="pool", d2d=True)
        P.dma(self.WUQ, self.m_wuq.rearrange("(kc p) n -> p kc n", p=128), writes=[self.r_W1], eng="pool", d2d=True)
        P.dma(self.WUKV, self.m_wukv.rearrange("(kc p) n -> p kc n", p=128), writes=[self.r_W1], eng="pool", d2d=True)
        P.dma(self.WO1, self.m_wout.rearrange("(kc p) n -> p kc n", p=128), writes=[self.r_W1], eng="pool", d2d=True)
        ffn(1)

    def phase_mod(self):
        P = self.P
        nj = self.nj
        with ExitStack() as es:
            cc = P.sbuf("cc", [128, 8, nj], F32, es)
            sc = P.sbuf("sc", [128, 8, nj], F32, es)
            bT = P.sbuf("bT", [128, 2, 48], F32, es)
            g1 = P.sbuf("g1", [128, 2, 8], F32, es)
            g2 = P.sbuf("g2", [128, 2, 8], F32, es)
            wt = [P.sbuf(f"adaw{i}", [128, 8, 512], F32, es) for i in range(2)]
            r_w = [P.res(), P.res()]
            r_cc, r_sc, r_bT, r_g = P.res(), P.res(), P.res(), P.res()
            P.dma(cc[:], self.ccT, writes=[r_cc])
            P.dma(bT[:], self.ada_bT, writes=[r_bT])
            P.dma(g1[:], self.n1g, writes=[r_g])
            P.dma(g2[:], self.n2g, writes=[r_g])
            P.memset(self.ONESb[:], 1.0, writes=[self.r_const], eng="dve")
            P.memset(self.EPSC[:, 0:1], EPS, writes=[self.r_const], eng="dve")
            P.memset(self.EPSC[:, 3:4], 1.0, writes=[self.r_const], eng="dve")
            P.act(sc[:], cc[:], AF.Silu, reads=[r_cc], writes=[r_sc])
            blk = 0
            for l in range(2):
                wsrc = self.ada_w[l].rearrange("(kc p) n -> p kc n", p=128)
                for nb_ in range(12):
                    s = blk % 2
                    P.dma(wt[s][:], wsrc[:, :, nb_ * 512:(nb_ + 1) * 512], writes=[r_w[s]])
                    pb = blk % 2
                    ps = self.bank(pb)
                    for mi in range(4):
                        for kc in range(8):
                            P.mm(ps[:, mi * nj:(mi + 1) * nj], wt[s][:, kc, mi * 128:(mi + 1) * 128], sc[:, kc, :],
                                 kc == 0, kc == 7, reads=[r_w[s], r_sc], writes=[self.r_ps[pb]])
                    for mi in range(4):
                        m = nb_ * 4 + mi
                        P.ts(self.MOD[:, l, m, :], ps[:, mi * nj:(mi + 1) * nj], bT[:, l, m:m + 1], None, ALU.add,
                             reads=[self.r_ps[pb], r_bT], writes=[self.r_MOD])
                    blk += 1
            for l in range(2):
                for t, (gt, v) in enumerate(((g1, 1), (g2, 4))):
                    for kc in range(8):
                        P.ts(self.GS[:, l, t, kc, :], self.MOD[:, l, v * 8 + kc, :], 1.0, gt[:, l, kc:kc + 1],
                             ALU.add, ALU.mult, reads=[self.r_MOD, r_g], writes=[self.r_MOD])
            P.barrier()

    def norm_alloc(self, es):
        P = self.P
        self.SQ = P.sbuf("SQ", [128, 8, 512], BF16, es)
        self.RS = [P.sbuf(f"RS{i}", [128, 512], F32, es) for i in range(2)]
        self.NT = [P.sbuf(f"NT{i}", [128, 512], F32, es) for i in range(2)]
        self.r_SQ = P.res()
        self.r_RS = [P.res(), P.res()]
        self.r_NT = [P.res(), P.res()]
        self.norm_i = 0
        self.nt_i = 0

    def norm_tile(self, R, r_R, a, b, HT, r_HT, off, l, t, j, psb):
        P = self.P
        n = b - a
        i = self.norm_i % 2
        self.norm_i += 1
        SQ, RS = self.SQ, self.RS[i]
        P.act(SQ[:, :, :n], R[:, :, a:b], AF.Square, reads=[r_R], writes=[self.r_SQ])
        ps = self.bank(psb)
        for kc in range(8):
            P.mm(ps[:, :n], self.ONESb[:], SQ[:, kc, :n], kc == 0, kc == 7,
                 reads=[self.r_SQ, self.r_const], writes=[self.r_ps[psb]])
        P.act(RS[:, :n], ps[:, :n], AF.Sqrt, reads=[self.r_ps[psb]], writes=[self.r_RS[i]],
              scale=1.0 / D, bias=self.eps_ap(EPS))
        P.recip(RS[:, :n], RS[:, :n], reads=[self.r_RS[i]], writes=[self.r_RS[i]])
        shv = 0 if t == 0 else 3
        for kc in range(8):
            k = self.nt_i % 2
            self.nt_i += 1
            NT = self.NT[k]
            P.stt(NT[:, :n], R[:, kc, a:b], self.GS[:, l, t, kc, j:j + 1], RS[:, :n], ALU.mult, ALU.mult,
                  reads=[r_R, self.r_RS[i], self.r_MOD], writes=[self.r_NT[k]])
            P.act(HT[:, kc, off:off + n], NT[:, :n], AF.Identity, reads=[self.r_NT[k], self.r_MOD], writes=[r_HT],
                  bias=self.MOD[:, l, shv * 8 + kc, j:j + 1], scale=1.0)

    def eps_ap(self, v):
        return self.EPSC[:, 0:1]

    def ffn(self, l, j, which):
        P = self.P
        R, r_R, L = (self.Rl, self.r_Rl, SEQ) if which == "l" else (self.Rc, self.r_Rc, CTX)
        with ExitStack() as es:
            self.norm_alloc(es)
            WDs = P.sbuf("WDs", [128, 22, D], BF16, es)
            r_WDs = P.res()
            WUs = [P.sbuf(f"WUs{i}", [128, 8, 2, 128], BF16, es) for i in range(3)]
            r_WUs = [P.res() for _ in range(3)]
            HT = [P.sbuf(f"HTf{i}", [128, 8, 512], BF16, es) for i in range(2)]
            r_HT = [P.res(), P.res()]
            WMAX = 410
            Hs = [P.sbuf(f"H{i}", [128, 22, WMAX], BF16, es) for i in range(2)]
            r_Hs = [P.res(), P.res()]
            A1 = [[P.sbuf(f"A1_{i}{g}", [128, WMAX], F32, es) for g in range(2)] for i in range(2)]
            r_A1 = [[P.res(), P.res()] for _ in range(2)]
            CW = P.sbuf("CW", [128, 44, 3], F32, es)
            r_CW = P.res()
            P.dma(CW[:], self.ffn_cw[:, l], writes=[r_CW])
            for c0 in range(0, 22, 2):
                P.dma(WDs[:, c0:c0 + 2, :], self.WD[l][:, c0:c0 + 2, :], reads=[self.r_WD[l]], writes=[r_WDs])
            wins = []
            t = 0
            while t < L:
                n = min(WMAX, L - t)
                wins.append((t, n))
                t += n
            pair_i = 0
            g2v = 5

            def down(o, wi):
                t0, n_out = wins[wi]
                H, r_H = Hs[wi % 2], r_Hs[wi % 2]
                pb = 6 + (o % 2)
                ps = self.bank(pb)
                for c in range(22):
                    P.mm(ps[:, :n_out], WDs[:, c, o * 128:(o + 1) * 128], H[:, c, :n_out], c == 0, c == 21,
                         reads=[r_WDs, r_H], writes=[self.r_ps[pb]])
                P.stt(R[:, o, t0:t0 + n_out], ps[:, :n_out], self.MOD[:, l, g2v * 8 + o, j:j + 1],
                      R[:, o, t0:t0 + n_out], ALU.mult, ALU.add,
                      reads=[self.r_ps[pb], r_R, self.r_MOD], writes=[r_R])

            for wi, (t0, n_out) in enumerate(wins):
                H, r_H = Hs[wi % 2], r_Hs[wi % 2]
                n_in = n_out + 2
                ht, r_ht = HT[wi % 2], r_HT[wi % 2]
                a = t0
                b = min(t0 + n_out + 1, L)
                off = 1
                if t0 == 0:
                    P.memset(ht[:, :, 0:1], 0.0, writes=[r_ht])
                else:
                    pht, r_pht = HT[(wi - 1) % 2], r_HT[(wi - 1) % 2]
                    pn = wins[wi - 1][1]
                    P.copy(ht[:, :, 0:1], pht[:, :, pn:pn + 1], reads=[r_pht], writes=[r_ht], eng="pool")
                if t0 + n_out == L:
                    P.memset(ht[:, :, n_in - 1:n_in], 0.0, writes=[r_ht])
                self.norm_tile(R, r_R, a, b, ht, r_ht, off, l, 1, j, 6)
                for c in range(22):
                    s = pair_i % 3
                    P.dma(WUs[s][:], self.WU[l, c], reads=[self.r_WU[l]], writes=[r_WUs[s]])
                    pb = 2 * (pair_i % 3)
                    ai = pair_i % 2
                    pair_i += 1
                    for gu in range(2):
                        ps = self.bank(pb + gu)
                        for kc in range(8):
                            P.mm(ps[:, :n_in], WUs[s][:, kc, gu, :], ht[:, kc, :n_in], kc == 0, kc == 7,
                                 reads=[r_WUs[s], r_ht], writes=[self.r_ps[pb + gu]])
                    for gu in range(2):
                        ps = self.bank(pb + gu)
                        ch = c + 22 * gu
                        A = A1[ai][gu]
                        rA = r_A1[ai][gu]
                        rp = self.r_ps[pb + gu]
                        P.act(A[:, :n_out], ps[:, 1:1 + n_out], AF.Identity, reads=[rp, r_CW], writes=[rA],
                              scale=CW[:, ch, 1:2])
                        P.stt(A[:, :n_out], ps[:, 0:n_out], CW[:, ch, 0:1], A[:, :n_out], ALU.mult, ALU.add,
                              reads=[rp, rA, r_CW], writes=[rA])
                        P.stt(A[:, :n_out], ps[:, 2:2 + n_out], CW[:, ch, 2:3], A[:, :n_out], ALU.mult, ALU.add,
                              reads=[rp, rA, r_CW], writes=[rA])
                    Ag, Au = A1[ai]
                    P.act(Ag[:, :n_out], Ag[:, :n_out], AF.Silu, reads=[r_A1[ai][0]], writes=[r_A1[ai][0]])
                    P.tt(H[:, c, :n_out], Ag[:, :n_out], Au[:, :n_out], ALU.mult,
                         reads=[r_A1[ai][0], r_A1[ai][1]], writes=[r_H], eng="pool")
                    if wi >= 1 and c >= 4 and c % 2 == 0 and (c - 4) // 2 < 8:
                        down((c - 4) // 2, wi - 1)
            for o in range(8):
                down(o, len(wins) - 1)
            P.barrier()

    def ap(self, T, off, dims):
        rowlen = 1
        for d in T.shape[1:]:
            rowlen *= d
        return bass.AP(T, off, [[rowlen, dims[0]]] + [list(d) for d in dims[1:]])

    def ab_proj(self, bi):
        P = self.P
        l = 0
        with ExitStack() as es:
            self.norm_alloc(es)
            WIN = P.sbuf("WINa", [128, 8, 3088], BF16, es)
            r_WIN = P.res()
            for c0 in range(0, 3088, 772):
                P.dma(WIN[:, :, c0:c0 + 772], self.WINA[:, :, c0:c0 + 772], reads=[self.r_WA], writes=[r_WIN])
            LNG = P.sbuf("LNG", [128, 512], F32, es)
            LNB = P.sbuf("LNB", [128, 512], F32, es)
            WST = P.sbuf("WST", [128, 4, 128], BF16, es)
            BSB = P.sbuf("BSB", [128, 4, 128], F32, es)
            BCW = P.sbuf("BCW", [128, 12, 3], F32, es)
            ALG = P.sbuf("ALG", [128, 8], F32, es)
            DTB = P.sbuf("DTB", [128, 8], F32, es)
            IDN = P.sbuf("IDN", [128, 128], F32, es)
            r_c = P.res()
            P.dma(LNG[:], self.a_lng, writes=[r_c])
            P.dma(LNB[:], self.a_lnb, writes=[r_c])
            P.dma(WST[:], self.a_wsT, writes=[r_c], eng="pool")
            P.dma(BSB[:], self.a_bsb, writes=[r_c])
            P.dma(BCW[:], self.b_cw, writes=[r_c])
            P.dma(ALG[:], self.b_alog, writes=[r_c])
            P.dma(DTB[:], self.b_dtb, writes=[r_c])
            P.dma(IDN[:], self.ident, writes=[r_c])
            P.act(ALG[:], ALG[:], AF.Exp, reads=[r_c], writes=[r_c])
            P.ts(ALG[:], ALG[:], -1.0, None, ALU.mult, reads=[r_c], writes=[r_c])
            HT = P.sbuf("HTa", [128, 8, 386], BF16, es)
            r_HT = P.res()
            AU = P.sbuf("AU", [128, 4, 384], BF16, es)
            r_AU = P.res()
            GT = P.sbuf("GTs", [128, 4, 384], BF16, es)
            r_GT = P.res()
            YAs = P.sbuf("YAs", [128, 4, 384], BF16, es)
            r_YAs = P.res()
            CV = [P.sbuf(f"CV{i}", [128, 384], F32, es) for i in range(2)]
            r_CV = [P.res(), P.res()]
            ST = [P.sbuf(f"ST{i}", [128, 1552], F32, es) for i in range(3)]
            r_ST = [P.res() for _ in range(3)]
            r_STg = [P.res() for _ in range(3)]
            r_t3 = P.res()
            SQT = P.sbuf("SQT", [128, 1024], F32, es)
            SS = P.sbuf("SSt", [128, 16], F32, es)
            r_t = P.res()
            GV = P.sbuf("GV", [128, 512], F32, es)
            GV2 = P.sbuf("GV2", [128, 512], F32, es)
            VN = P.sbuf("VNb", [128, 512], BF16, es)
            r_GV, r_VN = P.res(), P.res()
            TMPY = P.sbuf("TMPY", [128, 512], F32, es)
            r_TMPY = P.res()
            AB8 = P.sbuf("AB8", [128, 16], F32, es)
            r_AB8 = P.res()
            wins = [("c", 0, CTX)] + [("l", t0, min(384, SEQ - t0)) for t0 in range(0, SEQ, 384)]
            for which, t0, n_out in wins:
                R, r_R, L, jj, g0 = (self.Rl, self.r_Rl, SEQ, bi, CTX) if which == "l" else (self.Rc, self.r_Rc, CTX, self.nb, 0)
                n_in = n_out + 2
                a = max(t0 - 1, 0)
                b = min(t0 + n_out + 1, L)
                off = a - (t0 - 1)
                if t0 == 0:
                    P.memset(HT[:, :, 0:1], 0.0, writes=[r_HT], eng="dve")
                if t0 + n_out == L:
                    P.memset(HT[:, :, n_in - 1:n_in], 0.0, writes=[r_HT], eng="dve")
                self.norm_tile(R, r_R, a, b, HT, r_HT, off, l, 0, jj, 7)
                tokg = g0 + t0
                ntb = n_out // 128
                for g in range(4):
                    for kind, col0, func, DST, r_DST in ((0, g * 128, AF.Gelu, AU, r_AU), (1, 2560 + g * 128, AF.Silu, GT, r_GT)):
                        pb = (2 * g + kind) % 3
                        ps = self.bank(pb)
                        for kc in range(8):
                            P.mm(ps[:, :n_out], WIN[:, kc, col0:col0 + 128], HT[:, kc, 1:1 + n_out], kc == 0, kc == 7,
                                 reads=[r_WIN, r_HT], writes=[self.r_ps[pb]])
                        P.act(DST[:, g, :n_out], ps[:, :n_out], func, reads=[self.r_ps[pb]], writes=[r_DST])
                P.dma(self.GATE[:, :, tokg:tokg + n_out], GT[:, :, :n_out], reads=[r_GT], writes=[self.r_GATE])
                def transposes(ci, cv, r_cv):
                    kind, hh = ci // 4, ci % 4
                    for tb in range(ntb):
                        pt = self.bank(3 + tb)
                        P.mm(pt[:, hh * 128:(hh + 1) * 128], cv[:, tb * 128:(tb + 1) * 128], IDN[:], True, True,
                             reads=[r_cv, r_c], writes=[self.r_ps[3 + tb]])
                    if hh == 3:
                        for tb in range(ntb):
                            pt = self.bank(3 + tb)
                            P.copy(ST[tb][:, kind * 512:(kind + 1) * 512], pt[:, :], reads=[self.r_ps[3 + tb]],
                                   writes=[r_ST[tb]], eng="act" if tb == 1 else "dve")
                prev = None
                for ci in range(12):
                    pb = ci % 3
                    ps = self.bank(pb)
                    for kc in range(8):
                        P.mm(ps[:, :n_in], WIN[:, kc, 1024 + ci * 128:1024 + (ci + 1) * 128], HT[:, kc, :n_in], kc == 0, kc == 7,
                             reads=[r_WIN, r_HT], writes=[self.r_ps[pb]])
                    if prev is not None:
                        transposes(*prev)
                    cv, r_cv = CV[ci % 2], r_CV[ci % 2]
                    rp = self.r_ps[pb]
                    P.act(cv[:, :n_out], ps[:, 1:1 + n_out], AF.Identity, reads=[rp, r_c], writes=[r_cv], scale=BCW[:, ci, 1:2])
                    P.stt(cv[:, :n_out], ps[:, 0:n_out], BCW[:, ci, 0:1], cv[:, :n_out], ALU.mult, ALU.add,
                          reads=[rp, r_cv, r_c], writes=[r_cv])
                    P.stt(cv[:, :n_out], ps[:, 2:2 + n_out], BCW[:, ci, 2:3], cv[:, :n_out], ALU.mult, ALU.add,
                          reads=[rp, r_cv, r_c], writes=[r_cv])
                    P.act(cv[:, :n_out], cv[:, :n_out], AF.Silu, reads=[r_cv], writes=[r_cv])
                    prev = (ci, cv, r_cv)
                transposes(*prev)
                def chain_l2(tb):
                    st, r_st = ST[tb], r_ST[tb]
                    P.tt(SQT[:], st[:, 0:1024], st[:, 0:1024], ALU.mult, reads=[r_st], writes=[r_t])
                    yield
                    P.op("dve", lambda e: e.tensor_reduce(SS[:, 0:8], SQT[:].rearrange("p (u d) -> p u d", u=8),
                                                          AX.X, ALU.add), reads=[r_t], writes=[r_t])
                    yield
                    P.act(SS[:, 0:8], SS[:, 0:8], AF.Sqrt, reads=[r_t, self.r_const], writes=[r_t], bias=self.EPSC[:, 0:1], scale=1.0)
                    yield
                    P.recip(SS[:, 0:8], SS[:, 0:8], reads=[r_t], writes=[r_t])
                    P.ts(SS[:, 0:4], SS[:, 0:4], 128.0 ** -0.5, None, ALU.mult, reads=[r_t], writes=[r_t])
                    yield
                    P.tt(st[:, 0:1024].rearrange("p (u d) -> p u d", u=8), st[:, 0:1024].rearrange("p (u d) -> p u d", u=8),
                         self.ap(SS, 0, [128, [1, 8], [0, 128]]), ALU.mult,
                         reads=[r_st, r_t], writes=[r_st])

                def chain_ab(tb):
                    st, r_sg = ST[tb], r_STg[tb]
                    pa = self.bank(6)
                    c1 = 1 + tb * 128
                    for kc in range(8):
                        P.mm(pa[:, 0:16], HT[:, kc, c1:c1 + 128], WIN[:, kc, 3072:3088], kc == 0, kc == 7,
                             reads=[r_WIN, r_HT], writes=[self.r_ps[6]])
                    yield
                    P.tt(AB8[:, 0:8], pa[:, 0:8], DTB[:], ALU.add, reads=[self.r_ps[6], r_c], writes=[r_AB8])
                    P.act(st[:, 1544:1552], pa[:, 8:16], AF.Sigmoid, reads=[self.r_ps[6]], writes=[r_sg])
                    yield
                    P.act(AB8[:, 0:8], AB8[:, 0:8], AF.Exp, reads=[r_AB8], writes=[r_AB8])
                    yield
                    P.act(AB8[:, 0:8], AB8[:, 0:8], AF.Ln, reads=[r_AB8, self.r_const], writes=[r_AB8], bias=self.EPSC[:, 3:4], scale=1.0)
                    yield
                    P.tt(st[:, 1536:1544], AB8[:, 0:8], ALG[:], ALU.mult, reads=[r_AB8, r_c], writes=[r_sg])

                def chain_av(tb):
                    c1 = 1 + tb * 128
                    pv = self.bank(7)
                    for kc in range(8):
                        P.mm(pv[:, :], HT[:, kc, c1:c1 + 128], WIN[:, kc, 512:1024], kc == 0, kc == 7,
                             reads=[r_WIN, r_HT], writes=[self.r_ps[7]])
                    yield
                    P.act(GV[:], pv[:, :], AF.Gelu, reads=[self.r_ps[7]], writes=[r_GV])
                    yield
                    P.op("dve", lambda e: e.tensor_reduce(SS[:, 8:9], GV[:], AX.X, ALU.add), reads=[r_GV], writes=[r_t3])
                    P.ts(SS[:, 8:9], SS[:, 8:9], 1.0 / 512, None, ALU.mult, reads=[r_t3], writes=[r_t3])
                    yield
                    P.ts(GV[:], GV[:], SS[:, 8:9], None, ALU.subtract, reads=[r_GV, r_t3], writes=[r_GV])
                    yield
                    P.tt(GV2[:], GV[:], GV[:], ALU.mult, reads=[r_GV], writes=[r_VN])
                    yield
                    P.op("dve", lambda e: e.tensor_reduce(SS[:, 9:10], GV2[:], AX.X, ALU.add), reads=[r_VN], writes=[r_t3])
                    yield
                    P.act(SS[:, 9:10], SS[:, 9:10], AF.Sqrt, reads=[r_t3, self.r_const], writes=[r_t3], bias=self.EPSC[:, 0:1], scale=1.0 / 512)
                    yield
                    P.recip(SS[:, 9:10], SS[:, 9:10], reads=[r_t3], writes=[r_t3])
                    yield
                    P.stt(GV2[:], GV[:], SS[:, 9:10], LNG[:], ALU.mult, ALU.mult, reads=[r_GV, r_t3, r_c], writes=[r_VN])
                    yield
                    P.tt(VN[:], GV2[:], LNB[:], ALU.add, reads=[r_VN, r_c], writes=[r_VN])
                    py = self.bank(7)
                    for g in range(4):
                        P.mm(py[:, g * 128:(g + 1) * 128], VN[:, g * 128:(g + 1) * 128], WST[:, g, :], True, True,
                             reads=[r_VN, r_c], writes=[self.r_ps[7]])
                    yield
                    P.tt(TMPY[:], py[:, :], BSB[:].rearrange("p g i -> p (g i)"), ALU.add, reads=[self.r_ps[7], r_c], writes=[r_TMPY])
                    yield
                    P.tt(YAs[:, :, tb * 128:(tb + 1) * 128], TMPY[:].rearrange("p (g i) -> p g i", g=4),
                         AU[:, :, tb * 128:(tb + 1) * 128], ALU.mult, reads=[r_TMPY, r_AU], writes=[r_YAs])

                for tb in range(ntb):
                    gens = [chain_av(tb), chain_l2(tb), chain_ab(tb)]
                    while gens:
                        for g_ in list(gens):
                            try:
                                next(g_)
                            except StopIteration:
                                gens.remove(g_)
                    P.dma(self.QKV[tokg + tb * 128:tokg + (tb + 1) * 128, :], ST[tb][:], reads=[r_ST[tb], r_STg[tb]],
                          writes=[self.r_QKV])
                P.dma(self.YA[:, :, tokg:tokg + n_out], YAs[:, :, :n_out], reads=[r_YAs], writes=[self.r_YA])
            P.barrier()

    def gdn_scan(self, bi, O, r_O):
        P = self.P
        with ExitStack() as es:
            def T(name, shape, dt=F32):
                return P.sbuf(name, shape, dt, es)

            def T2(name, shape, dt=F32):
                return [P.sbuf(f"{name}{i}", shape, dt, es) for i in range(2)]

            def R2():
                return [P.res(), P.res()]
            IDN = T("IDNg", [128, 128])
            MC = T("MC", [64, 8, 64])
            MS = T("MS", [64, 8, 64])
            MSi = T("MSi", [64, 8, 64])
            MN = T("MN", [64, 8, 64])
            MNi = T("MNi", [64, 8, 64])
            M2 = T("M2", [64, 2, 64])
            S2 = T("S2", [64, 2, 64])
            ON = T("ONf", [64, 128])
            r_c = P.res()
            P.dma(IDN[:], self.ident, writes=[r_c])
            P.dma(M2[:], self.maskc, writes=[r_c])
            P.dma(S2[:], self.strict, writes=[r_c])
            P.memset(ON[:], 1.0, writes=[r_c])
            for d in range(2):
                for (dst, src, dd) in ((MC, M2, d), (MS, S2, d), (MNi, M2, 1 - d), (MSi, S2, 1 - d)):
                    P.copy(dst[:, 4 * d:4 * d + 4, :], self.ap(src, dd * 64, [64, [0, 4], [1, 64]]), reads=[r_c], writes=[r_c])
            P.ts(MN[:], MC[:], -1.0, 30000.0, ALU.add, ALU.mult, reads=[r_c], writes=[r_c])
            P.ts(MNi[:], MNi[:], -1.0, 30000.0, ALU.add, ALU.mult, reads=[r_c], writes=[r_c])
            IDB = T("IDB", [64, 64], BF16)
            P.copy(IDB[:], IDN[0:64, 0:64], reads=[r_c], writes=[r_c])
            X = T2("X", [64, 2, 1552])
            r_X = R2()
            SM = T2("SM", [64, 6, 8])
            EG = T2("EG", [128, 8])
            r_s = R2()
            DEC, DECi = T("DEC", [64, 8, 64]), T("DECi", [64, 8, 64])
            Gm = DECi
            r_DEC = P.res()
            r_Gm = r_DEC
            KB, QG = T("KB", [64, 8, 128], BF16), T("QG", [64, 8, 128], BF16)
            r_kq = P.res()
            KBG, VB, KDEC = T2("KBG", [64, 8, 128], BF16), T2("VB", [64, 8, 128], BF16), T2("KDEC", [64, 8, 128], BF16)
            r_tm = R2()
            X16 = T("X16", [64, 2, 1024], BF16)
            r_X16 = P.res()
            KT, KBT, QT = [T(n, [128, 8, 64], BF16) for n in ("KT", "KBT", "QT")]
            r_fm = P.res()
            QGT = T2("QGT", [128, 8, 64], BF16)
            r_QGT = R2()
            X0, Y0 = T2("X0", [64, 8, 64]), T2("Y0", [64, 8, 64])
            r_X0, r_Y0 = R2(), R2()
            ATT = T2("ATT", [64, 8, 64], BF16)
            r_ATT = R2()
            Xa, Xb, Ya, Yb, Pm = [T(n, [64, 8, 64]) for n in ("Xa", "Xb", "Ya", "Yb", "Pm")]
            r_Xa, r_Xb, r_Ya, r_Yb, r_Pm = [P.res() for _ in range(5)]
            Pm16 = T("Pm16", [64, 8, 64], BF16)
            r_Pm16 = P.res()
            NWT = T("NWT", [128, 8, 64], BF16)
            r_NWT = P.res()
            VNEW = T("VNEW", [64, 8, 128], BF16)
            r_VNEW = P.res()
            S = T("S", [128, 8, 128])
            S16 = T("S16", [128, 8, 128], BF16)
            r_S, r_S16 = P.res(), P.res()
            P.memset(S[:], 0.0, writes=[r_S])
            P.memset(S16[:], 0.0, writes=[r_S16])
            P.memset(O[:], 0.0, writes=[r_O])
            bk = self.bank
            rp = self.r_ps
            NS = 36

            def b3(ps):
                return ps.rearrange("p (u i) -> p u i", u=8)

            def chunks(n):
                return n, ((3 - n) if n < 4 else (39 - n))

            def load(n):
                if n >= NS:
                    return
                cf, cb = chunks(n)
                x, r_x = X[n % 2], r_X[n % 2]
                P.dma(x[:, 0, :], self.QKV[cf * 64:(cf + 1) * 64, :], reads=[self.r_QKV], writes=[r_x])
                P.dma(x[:, 1, :], self.QKV[cb * 64:(cb + 1) * 64, :], reads=[self.r_QKV], writes=[r_x])

            def prologue(n):
                p = n % 2
                x, r_x = X[p], r_X[p]
                sm, rs = SM[p], r_s[p]
                G8, B8, GC, E1, E2, BE1 = [sm[:, i, :] for i in range(6)]
                eg = EG[p]
                load(n + 1)

                def xv(kind):
                    return self.ap(x, kind * 512, [64, [1552, 2], [128, 4], [1, 128]])

                def xu(kind, u):
                    return self.ap(X16, (u // 4) * 1024 + kind * 512 + (u % 4) * 128, [64, [1, 128]])

                def v4(t):
                    return t[:].rearrange("p (d h) k -> p d h k", d=2)

                def smb(i, inner):
                    return self.ap(sm, i * 8, [64, [1, 8], [0, inner]])

                def smb4(i):
                    return self.ap(sm, i * 8, [64, [4, 2], [1, 4], [0, 128]])
                P.copy(X16[:], x[:, :, 0:1024], reads=[r_x], writes=[r_X16], eng="pool")
                P.copy(self.ap(sm, 0, [64, [4, 2], [1, 4]]), self.ap(x, 1536, [64, [1552 + 4, 2], [1, 4]]), reads=[r_x], writes=[rs])
                P.copy(self.ap(sm, 8, [64, [4, 2], [1, 4]]), self.ap(x, 1544, [64, [1552 + 4, 2], [1, 4]]), reads=[r_x], writes=[rs])
                p0 = bk(0)
                for d in range(2):
                    P.mm(p0[0:64, 4 * d:4 * d + 4], M2[:, d, :], G8[:, 4 * d:4 * d + 4], True, True, reads=[r_c, rs], writes=[rp[0]])
                P.mm(p0[:, 8:16], ON[:], G8, True, True, reads=[r_c, rs], writes=[rp[0]])
                yield
                P.copy(GC, p0[0:64, 0:8], reads=[rp[0]], writes=[rs])
                P.act(E1, p0[0:64, 0:8], AF.Exp, reads=[rp[0]], writes=[rs])
                P.tt(E2, p0[0:64, 8:16], GC, ALU.subtract, reads=[rp[0], rs], writes=[rs])
                P.act(E2, E2, AF.Exp, reads=[rs], writes=[rs])
                P.act(eg[:], p0[:, 8:16], AF.Exp, reads=[rp[0]], writes=[rs])
                P.tt(BE1, B8, E1, ALU.mult, reads=[rs], writes=[rs])
                P.tt(Gm[:], MC[:], smb(0, 64), ALU.mult, reads=[r_c, rs], writes=[r_Gm])
                p1 = bk(4)
                P.mm(p1[0:64, :], ON[:, 0:64], Gm[:].rearrange("p u i -> p (u i)"), True, True, reads=[r_c, r_Gm], writes=[rp[4]])
                yield
                gcb = smb(2, 64)
                P.tt(DEC[:], b3(p1[0:64, :]), MN[:], ALU.add, reads=[rp[4], r_c], writes=[r_DEC])
                P.tt(DEC[:], DEC[:], gcb, ALU.subtract, reads=[r_DEC, rs], writes=[r_DEC])
                P.act(DEC[:], DEC[:], AF.Exp, reads=[r_DEC], writes=[r_DEC])
                yield
                P.tt(DECi[:], gcb, b3(p1[0:64, :]), ALU.subtract, reads=[rp[4], rs], writes=[r_DEC])
                P.tt(DECi[:], DECi[:], MNi[:], ALU.add, reads=[r_DEC, r_c], writes=[r_DEC])
                P.act(DECi[:], DECi[:], AF.Exp, reads=[r_DEC], writes=[r_DEC])
                yield
                P.tt(v4(KB), xv(1), smb4(1), ALU.mult, reads=[r_x, rs], writes=[r_kq])
                P.tt(v4(QG), xv(0), smb4(3), ALU.mult, reads=[r_x, rs], writes=[r_kq])
                yield
                P.tt(v4(KBG[p]), xv(1), smb4(5), ALU.mult, reads=[r_x, rs], writes=[r_tm[p]])
                P.tt(v4(VB[p]), xv(2), smb4(1), ALU.mult, reads=[r_x, rs], writes=[r_tm[p]])
                P.tt(v4(KDEC[p]), xv(1), smb4(4), ALU.mult, reads=[r_x, rs], writes=[r_tm[p]])
                yield
                for bi_, (src, dst, r_dst, pb) in enumerate(((1, KT, r_fm, 5), (KB, KBT, r_fm, 6), (0, QT, r_fm, 7),
                                                             (QG, QGT[p], r_QGT[p], 4))):
                    ps = bk(pb)
                    for u in range(8):
                        sap = xu(src, u) if isinstance(src, int) else src[:, u, :]
                        P.mm(ps[:, u * 64:(u + 1) * 64], sap, IDB[:], True, True,
                             reads=[r_X16, r_kq, r_c], writes=[rp[pb]])
                    P.copy(dst[:], b3(ps[:, :]), reads=[rp[pb]], writes=[r_dst], eng="act" if bi_ % 2 else "dve")
                    yield
                pA, pAi, pAt = bk(5), bk(6), bk(7)
                for u in range(8):
                    P.mm(pA[0:64, u * 64:(u + 1) * 64], KT[:, u, :], KBT[:, u, :], True, True, reads=[r_fm], writes=[rp[5]])
                for u in range(8):
                    P.mm(pAi[0:64, u * 64:(u + 1) * 64], KBT[:, u, :], KT[:, u, :], True, True, reads=[r_fm], writes=[rp[6]])
                for u in range(8):
                    P.mm(pAt[0:64, u * 64:(u + 1) * 64], KT[:, u, :], QT[:, u, :], True, True, reads=[r_fm], writes=[rp[7]])
                yield
                x0, y0 = X0[p], Y0[p]
                P.stt(x0[:], b3(pA[0:64, :]), -1.0, DEC[:], ALU.mult, ALU.mult, reads=[rp[5], r_DEC], writes=[r_X0[p]])
                P.tt(x0[:], x0[:], MS[:], ALU.mult, reads=[r_X0[p], r_c], writes=[r_X0[p]])
                yield
                P.stt(y0[:], b3(pAi[0:64, :]), -1.0, DECi[:], ALU.mult, ALU.mult, reads=[rp[6], r_DEC], writes=[r_Y0[p]])
                P.tt(y0[:], y0[:], MSi[:], ALU.mult, reads=[r_Y0[p], r_c], writes=[r_Y0[p]])
                P.tt(ATT[p][:], b3(pAt[0:64, :]), DEC[:], ALU.mult, reads=[rp[7], r_DEC], writes=[r_ATT[p]])
                yield

            def inverse(n, tg, nxt):
                p = n % 2
                P.tt(Pm[:], X0[p][:], self.ap(IDN, 0, [64, [0, 8], [1, 64]]), ALU.add, reads=[r_X0[p], r_c], writes=[r_Pm])
                Xc, r_Xc, Yc, r_Yc = X0[p], r_X0[p], Y0[p], r_Y0[p]
                tgt = [(Xa, r_Xa, Ya, r_Ya), (Xb, r_Xb, Yb, r_Yb)]
                for lev in range(1, 6):
                    Xn, r_Xn, Yn, r_Yn = tgt[lev % 2]
                    py, px, pp = bk(1), bk(2), bk(3)
                    for u in range(8):
                        P.mm(py[0:64, u * 64:(u + 1) * 64], Xc[:, u, :], Yc[:, u, :], True, True, reads=[r_Xc, r_Yc], writes=[rp[1]])
                    P.copy(Yn[:], b3(py[0:64, :]), reads=[rp[1]], writes=[r_Yn], eng="act")
                    if lev < 5:
                        for u in range(8):
                            P.mm(px[0:64, u * 64:(u + 1) * 64], Yc[:, u, :], Xc[:, u, :], True, True, reads=[r_Xc, r_Yc], writes=[rp[2]])
                        P.copy(Xn[:], b3(px[0:64, :]), reads=[rp[2]], writes=[r_Xn], eng="act")
                    for u in range(8):
                        P.mm(pp[0:64, u * 64:(u + 1) * 64], Yn[:, u, :], Pm[:, u, :], True, True, reads=[r_Yn, r_Pm], writes=[rp[3]])
                    P.tt(Pm[:], Pm[:], b3(pp[0:64, :]), ALU.add, reads=[rp[3], r_Pm], writes=[r_Pm])
                    Xc, r_Xc, Yc, r_Yc = Xn, r_Xn, Yn, r_Yn
                    for _ in range(5):
                        if next(tg, "end") == "end":
                            next(nxt, None)
                for _ in tg:
                    pass
                for _ in nxt:
                    pass
                P.copy(Pm16[:], Pm[:], reads=[r_Pm], writes=[r_Pm16], eng="act")

            def tail(n):
                p = n % 2
                cf, cb = chunks(n)
                pw = bk(4)
                for u in range(8):
                    P.mm(pw[:, u * 64:(u + 1) * 64], KBG[p][:, u, :], Pm16[:, u, :], True, True, reads=[r_tm[p], r_Pm16], writes=[rp[4]])
                P.ts(NWT[:], b3(pw[:, :]), -1.0, None, ALU.mult, reads=[rp[4]], writes=[r_NWT])
                yield
                for u in range(8):
                    pb = 5 + u // 4
                    pv = bk(pb)
                    c0 = (u % 4) * 128
                    P.mm(pv[0:64, c0:c0 + 128], Pm16[:, u, :], VB[p][:, u, :], True, False, reads=[r_Pm16, r_tm[p]], writes=[rp[pb]])
                    P.mm(pv[0:64, c0:c0 + 128], NWT[:, u, :], S16[:, u, :], False, True, reads=[r_NWT, r_S16], writes=[rp[pb]])
                for hv in range(2):
                    P.copy(VNEW[:, 4 * hv:4 * hv + 4, :], bk(5 + hv)[0:64, :].rearrange("p (u d) -> p u d", u=4),
                           reads=[rp[5 + hv]], writes=[r_VNEW], eng="act" if hv else "dve")
                yield
                po = bk(7)
                for u in range(8):
                    P.mm(po[:, u * 64:(u + 1) * 64], S16[:, u, :], QGT[p][:, u, :], True, False, reads=[r_S16, r_QGT[p]], writes=[rp[7]])
                    P.mm(po[:, u * 64:(u + 1) * 64], VNEW[:, u, :], ATT[p][:, u, :], False, True, reads=[r_VNEW, r_ATT[p]], writes=[rp[7]])
                for d, ck in ((0, cf), (1, cb)):
                    P.tt(O[:, :, ck * 64:(ck + 1) * 64], O[:, :, ck * 64:(ck + 1) * 64],
                         po[:, 256 * d:256 * (d + 1)].rearrange("p (h i) -> p h i", h=4), ALU.add,
                         reads=[rp[7], r_O], writes=[r_O])
                yield
                SB = (0, 4)
                for hv in range(2):
                    pb = SB[hv]
                    psu = bk(pb)
                    for uu in range(4):
                        u = 4 * hv + uu
                        P.mm(psu[:, uu * 128:(uu + 1) * 128], KDEC[p][:, u, :], VNEW[:, u, :], True, True,
                             reads=[r_tm[p], r_VNEW], writes=[rp[pb]])
                P.tt(S[:], S[:], self.ap(EG[p], 0, [128, [1, 8], [0, 128]]), ALU.mult, reads=[r_S, r_s[p]], writes=[r_S])
                yield
                for hv in range(2):
                    P.tt(S[:, 4 * hv:4 * hv + 4, :], S[:, 4 * hv:4 * hv + 4, :],
                         bk(SB[hv])[:, :].rearrange("p (u d) -> p u d", u=4), ALU.add, reads=[rp[SB[hv]], r_S], writes=[r_S])
                P.copy(S16[:], S[:], reads=[r_S], writes=[r_S16], eng="act")
                yield

            load(0)
            for _ in prologue(0):
                pass
            for n in range(NS):
                tg = tail(n - 1) if n >= 1 else iter(())
                nxt = prologue(n + 1) if n + 1 < NS else iter(())
                inverse(n, tg, nxt)
            for _ in tail(NS - 1):
                pass
            P.barrier()

    def ab_out(self, bi, O, r_O):
        P = self.P
        l = 0
        with ExitStack() as es:
            WO = P.sbuf("WOa", [128, 8, D], BF16, es)
            r_WO = P.res()
            P.dma(WO[:], self.WOA, reads=[self.r_WA], writes=[r_WO])
            NG = P.sbuf("NGb", [128, 1], F32, es)
            r_NG = P.res()
            P.dma(NG[:], self.b_ng, writes=[r_NG])
            YAt = [P.sbuf(f"YAt{i}", [128, 4, 512], BF16, es) for i in range(2)]
            GTt = [P.sbuf(f"GTt{i}", [128, 4, 512], BF16, es) for i in range(2)]
            r_YAt, r_GTt = [P.res(), P.res()], [P.res(), P.res()]
            SQ = P.sbuf("SQo", [128, 4, 512], BF16, es)
            r_SQ = P.res()
            RS = P.sbuf("RSo", [128, 512], F32, es)
            r_RS = P.res()
            YB = P.sbuf("YB", [128, 4, 512], BF16, es)
            TMP = P.sbuf("TMPo", [128, 512], F32, es)
            r_YB, r_TMP = P.res(), P.res()
            tiles = [("c", 0, CTX)] + [("l", t * 512, (t + 1) * 512) for t in range(4)]
            for ti, (which, a, b) in enumerate(tiles):
                n = b - a
                R, r_R, jj, g0 = (self.Rl, self.r_Rl, bi, CTX) if which == "l" else (self.Rc, self.r_Rc, self.nb, 0)
                ga = g0 + a
                ya, gt = YAt[ti % 2], GTt[ti % 2]
                P.dma(ya[:, :, :n], self.YA[:, :, ga:ga + n], reads=[self.r_YA], writes=[r_YAt[ti % 2]])
                P.dma(gt[:, :, :n], self.GATE[:, :, ga:ga + n], reads=[self.r_GATE], writes=[r_GTt[ti % 2]])
                for h in range(4):
                    pb = h % 2
                    ps = self.bank(pb)
                    P.act(SQ[:, h, :n], O[:, h, ga:ga + n], AF.Square, reads=[r_O], writes=[r_SQ])
                    P.mm(ps[:, :n], self.ONESb[:], SQ[:, h, :n], True, True, reads=[r_SQ, self.r_const], writes=[self.r_ps[pb]])
                    P.act(RS[:, :n], ps[:, :n], AF.Sqrt, reads=[self.r_ps[pb], self.r_const], writes=[r_RS], scale=1.0 / 128, bias=self.EPSC[:, 0:1])
                    P.recip(RS[:, :n], RS[:, :n], reads=[r_RS], writes=[r_RS])
                    P.stt(TMP[:, :n], O[:, h, ga:ga + n], NG[:, 0:1], RS[:, :n], ALU.mult, ALU.mult, reads=[r_O, r_NG, r_RS], writes=[r_TMP])
                    P.tt(YB[:, h, :n], TMP[:, :n], gt[:, h, :n], ALU.mult, reads=[r_TMP, r_GTt[ti % 2]], writes=[r_YB])
                for o in range(8):
                    pb = 2 + o % 3
                    ps = self.bank(pb)
                    for kc in range(8):
                        src = ya[:, kc, :n] if kc < 4 else YB[:, kc - 4, :n]
                        P.mm(ps[:, :n], WO[:, kc, o * 128:(o + 1) * 128], src, kc == 0, kc == 7,
                             reads=[r_WO, r_YAt[ti % 2], r_YB], writes=[self.r_ps[pb]])
                    P.stt(R[:, o, a:b], ps[:, :n], self.MOD[:, l, 2 * 8 + o, jj:jj + 1], R[:, o, a:b], ALU.mult, ALU.add,
                          reads=[self.r_ps[pb], self.r_MOD, r_R], writes=[r_R])
            P.barrier()

    def ab(self, bi):
        P = self.P
        self.ab_proj(bi)
        with ExitStack() as es:
            O = P.sbuf("Og", [128, 4, CTX + SEQ], F32, es)
            r_O = P.res()
            self.gdn_scan(bi, O, r_O)
            self.ab_out(bi, O, r_O)

    def mla(self, bi):
        P = self.P
        j = bi
        jc = self.nb
        l = 1
        NK = CTX + SEQ
        scale = (128 + 64) ** -0.5
        with ExitStack() as es0:
            CQN = P.sbuf("CQN", [128, 3, SEQ], BF16, es0)
            CKVN = P.sbuf("CKVN", [128, 2, NK], BF16, es0)
            KR = P.sbuf("KR", [64, NK], BF16, es0)
            COS = P.sbuf("COS", [64, SEQ], F32, es0)
            SIN = P.sbuf("SIN", [64, SEQ], F32, es0)
            r_CQN, r_CKVN, r_KR, r_rope = P.res(), P.res(), P.res(), P.res()
            P.dma(COS[:], self.ropeC, writes=[r_rope])
            P.dma(SIN[:], self.ropeS, writes=[r_rope])
            with ExitStack() as es:
                self.norm_alloc(es)
                WIN = P.sbuf("WIN", [128, 8, 768], BF16, es)
                r_WIN = P.res()
                P.dma(WIN[:], self.WIN1, reads=[self.r_W1], writes=[r_WIN])
                NG = P.sbuf("NG", [128, 5], F32, es)
                r_NG = P.res()
                P.dma(NG[:, 0:3], self.m_qng, writes=[r_NG])
                P.dma(NG[:, 3:5], self.m_kvng, writes=[r_NG])
                HT = P.sbuf("HTm", [128, 8, 512], BF16, es)
                r_HT = P.res()
                SQ2 = P.sbuf("SQ2", [128, 3, 512], BF16, es)
                r_SQ2 = P.res()
                RS2 = P.sbuf("RS2", [128, 512], F32, es)
                r_RS2 = P.res()
                T1 = P.sbuf("T1", [64, 512], F32, es)
                T2 = P.sbuf("T2", [64, 512], F32, es)
                r_T = P.res()
                P.memset(self.EPSC[:, 1:2], 384.0 * EPS, writes=[self.r_const])
                P.memset(self.EPSC[:, 2:3], 256.0 * EPS, writes=[self.r_const])
                tiles = [("c", 0, CTX)] + [("l", t * 512, (t + 1) * 512) for t in range(4)]
                for which, a, b in tiles:
                    n = b - a
                    R, r_R, jj = (self.Rl, self.r_Rl, j) if which == "l" else (self.Rc, self.r_Rc, jc)
                    k0 = a + CTX if which == "l" else a
                    self.norm_tile(R, r_R, a, b, HT, r_HT, 0, l, 0, jj, 7)
                    groups = []
                    if which == "l":
                        groups.append((0, 3, CQN, r_CQN, a, 0, 1))
                    groups.append((3, 2, CKVN, r_CKVN, k0, 3, 2))
                    for c0, ncnk, DST, r_DST, d0, ng0, epsi in groups:
                        for ci in range(ncnk):
                            ps = self.bank(ci)
                            for kc in range(8):
                                P.mm(ps[:, :n], WIN[:, kc, (c0 + ci) * 128:(c0 + ci + 1) * 128], HT[:, kc, :n],
                                     kc == 0, kc == 7, reads=[r_WIN, r_HT], writes=[self.r_ps[ci]])
                            P.act(SQ2[:, ci, :n], ps[:, :n], AF.Square, reads=[self.r_ps[ci]], writes=[r_SQ2])
                        pss = self.bank(3)
                        for ci in range(ncnk):
                            P.mm(pss[:, :n], self.ONESb[:], SQ2[:, ci, :n], ci == 0, ci == ncnk - 1,
                                 reads=[r_SQ2, self.r_const], writes=[self.r_ps[3]])
                        P.act(RS2[:, :n], pss[:, :n], AF.Sqrt, reads=[self.r_ps[3], self.r_const], writes=[r_RS2],
                              scale=1.0 / (ncnk * 128), bias=self.EPSC[:, 0:1])
                        P.recip(RS2[:, :n], RS2[:, :n], reads=[r_RS2], writes=[r_RS2])
                        for ci in range(ncnk):
                            ps = self.bank(ci)
                            P.stt(DST[:, ci, d0:d0 + n], ps[:, :n], NG[:, ng0 + ci:ng0 + ci + 1], RS2[:, :n],
                                  ALU.mult, ALU.mult, reads=[self.r_ps[ci], r_NG, r_RS2], writes=[r_DST])
                    pk = self.bank(4)
                    pks = self.bank(5)
                    for kc in range(8):
                        P.mm(pk[0:64, :n], WIN[:, kc, 640:704], HT[:, kc, :n], kc == 0, kc == 7,
                             reads=[r_WIN, r_HT], writes=[self.r_ps[4]])
                    if which == "l":
                        for kc in range(8):
                            P.mm(pks[0:64, :n], WIN[:, kc, 704:768], HT[:, kc, :n], kc == 0, kc == 7,
                                 reads=[r_WIN, r_HT], writes=[self.r_ps[5]])
                        P.tt(T1[:, :n], pk[0:64, :n], COS[:, a:b], ALU.mult, reads=[self.r_ps[4], r_rope], writes=[r_T])
                        P.tt(T2[:, :n], pks[0:64, :n], SIN[:, a:b], ALU.mult, reads=[self.r_ps[5], r_rope, r_T], writes=[r_T])
                        P.tt(KR[:, k0:k0 + n], T1[:, :n], T2[:, :n], ALU.add, reads=[r_T], writes=[r_KR])
                    else:
                        P.copy(KR[:, k0:k0 + n], pk[0:64, :n], reads=[self.r_ps[4]], writes=[r_KR])
                P.barrier()
            with ExitStack() as es:
                WUQs = P.sbuf("WUQs", [128, 3, 2048], BF16, es)
                WUKVs = P.sbuf("WUKVs", [128, 2, 2048], BF16, es)
                WOs = P.sbuf("WOs", [128, 4, D], BF16, es)
                r_WOs = P.res()
                r_Ws = P.res()
                P.dma(WUQs[:], self.WUQ, reads=[self.r_W1], writes=[r_Ws])
                P.dma(WUKVs[:], self.WUKV, reads=[self.r_W1], writes=[r_Ws])
                KN = P.sbuf("KN", [128, 4, NK], BF16, es)
                V = P.sbuf("V", [128, 18, 512], BF16, es)
                r_KN, r_V = P.res(), P.res()
                QN = P.sbuf("QN", [128, 4, 512], BF16, es)
                QR = P.sbuf("QR", [64, 4, 512], BF16, es)
                r_QN, r_QR = P.res(), P.res()
                T1 = P.sbuf("T1b", [64, 512], F32, es)
                T2 = P.sbuf("T2b", [64, 512], F32, es)
                r_T = P.res()
                PT = [P.sbuf(f"PT{i}", [128, 512], BF16, es) for i in range(3)]
                r_PT = [P.res() for _ in range(3)]
                RSM = P.sbuf("RSM", [128, 512], F32, es)
                r_RSM = P.res()
                OA = P.sbuf("OA", [128, 4, 512], BF16, es)
                r_OA = P.res()
                pti = 0
                for hg in range(2):
                    P.dma(WOs[:], self.WO1[:, 4 * hg:4 * hg + 4, :], reads=[self.r_W1], writes=[r_WOs])
                    ktiles = [(0, CTX)] + [(CTX + t * 512, CTX + (t + 1) * 512) for t in range(4)]
                    cnt = 0
                    for hh in range(4):
                        h = 4 * hg + hh
                        for (ka, kb) in ktiles:
                            n = kb - ka
                            pb = 5 + cnt % 3
                            cnt += 1
                            ps = self.bank(pb)
                            for c in range(2):
                                P.mm(ps[:, :n], WUKVs[:, c, h * 128:(h + 1) * 128], CKVN[:, c, ka:kb], c == 0, c == 1,
                                     reads=[r_Ws, r_CKVN], writes=[self.r_ps[pb]])
                            P.act(KN[:, hh, ka:kb], ps[:, :n], AF.Copy, reads=[self.r_ps[pb]], writes=[r_KN])
                    for kt in range(18):
                        pb = 5 + cnt % 3
                        cnt += 1
                        ps = self.bank(pb)
                        for c in range(2):
                            P.mm(ps[:, :], CKVN[:, c, kt * 128:(kt + 1) * 128],
                                 WUKVs[:, c, 1024 + 512 * hg:1024 + 512 * (hg + 1)], c == 0, c == 1,
                                 reads=[r_Ws, r_CKVN], writes=[self.r_ps[pb]])
                        P.copy(V[:, kt, :], ps[:, :], reads=[self.r_ps[pb]], writes=[r_V])
                    for qt in range(4):
                        qa, qb = qt * 512, (qt + 1) * 512
                        for hh in range(4):
                            h = 4 * hg + hh
                            pb = 5 + cnt % 3
                            cnt += 1
                            ps = self.bank(pb)
                            for c in range(3):
                                P.mm(ps[:, :], WUQs[:, c, h * 256:h * 256 + 128], CQN[:, c, qa:qb], c == 0, c == 2,
                                     reads=[r_Ws, r_CQN], writes=[self.r_ps[pb]])
                            P.act(QN[:, hh, :], ps[:, :], AF.Copy, reads=[self.r_ps[pb]], writes=[r_QN])
                            pb1 = 5 + cnt % 3
                            cnt += 1
                            pb2 = 5 + cnt % 3
                            cnt += 1
                            p1, p2 = self.bank(pb1), self.bank(pb2)
                            for c in range(3):
                                P.mm(p1[0:64, :], WUQs[:, c, h * 256 + 128:h * 256 + 192], CQN[:, c, qa:qb], c == 0, c == 2,
                                     reads=[r_Ws, r_CQN], writes=[self.r_ps[pb1]])
                            for c in range(3):
                                P.mm(p2[0:64, :], WUQs[:, c, h * 256 + 192:h * 256 + 256], CQN[:, c, qa:qb], c == 0, c == 2,
                                     reads=[r_Ws, r_CQN], writes=[self.r_ps[pb2]])
                            P.tt(T1[:, :], p1[0:64, :], COS[:, qa:qb], ALU.mult, reads=[self.r_ps[pb1], r_rope], writes=[r_T])
                            P.tt(T2[:, :], p2[0:64, :], SIN[:, qa:qb], ALU.mult, reads=[self.r_ps[pb2], r_rope, r_T], writes=[r_T])
                            P.tt(QR[:, hh, :], T1[:, :], T2[:, :], ALU.add, reads=[r_T], writes=[r_QR])
                        for hh in range(4):
                            bo, bs_ = (3, 4) if hh % 2 == 0 else (5, 6)
                            po, psm = self.bank(bo), self.bank(bs_)
                            def s_stage(kt, sb):
                                pss = self.bank(sb)
                                P.mm(pss[:, :], KN[:, hh, kt * 128:(kt + 1) * 128], QN[:, hh, :], True, False,
                                     reads=[r_KN, r_QN], writes=[self.r_ps[sb]])
                                P.mm(pss[:, :], KR[:, kt * 128:(kt + 1) * 128], QR[:, hh, :], False, True,
                                     reads=[r_KR, r_QR], writes=[self.r_ps[sb]])
                                P.act(PT[sb][:], pss[:, :], AF.Exp, reads=[self.r_ps[sb]], writes=[r_PT[sb]], scale=scale)

                            def pv_stage(kt, sb):
                                P.mm(po[:, :], V[:, kt, hh * 128:(hh + 1) * 128], PT[sb][:], kt == 0, kt == 17,
                                     reads=[r_V, r_PT[sb]], writes=[self.r_ps[bo]])
                                P.mm(psm[:, :], self.ONESb[:], PT[sb][:], kt == 0, kt == 17,
                                     reads=[r_PT[sb], self.r_const], writes=[self.r_ps[bs_]])
                            SK = 2
                            sbs = []
                            for kt in range(18 + SK):
                                if kt < 18:
                                    sbs.append(pti % 3)
                                    pti += 1
                                    s_stage(kt, sbs[kt])
                                if kt >= SK:
                                    pv_stage(kt - SK, sbs[kt - SK])
                            P.recip(RSM[:], psm[:, :], reads=[self.r_ps[bs_]], writes=[r_RSM])
                            P.tt(OA[:, hh, :], po[:, :], RSM[:], ALU.mult, reads=[self.r_ps[bo], r_RSM], writes=[r_OA])
                        for o in range(8):
                            pb = 5 + cnt % 3
                            cnt += 1
                            ps = self.bank(pb)
                            for hh in range(4):
                                P.mm(ps[:, :], WOs[:, hh, o * 128:(o + 1) * 128], OA[:, hh, :], hh == 0, hh == 3,
                                     reads=[r_WOs, r_OA], writes=[self.r_ps[pb]])
                            P.stt(self.Rl[:, o, qa:qb], ps[:, :], self.MOD[:, l, 2 * 8 + o, j:j + 1], self.Rl[:, o, qa:qb],
                                  ALU.mult, ALU.add, reads=[self.r_ps[pb], self.r_MOD, self.r_Rl], writes=[self.r_Rl])
                P.barrier()

    def final(self, bi):
        P = self.P
        R, r_R = self.Rl, self.r_Rl
        with ExitStack() as es:
            self.norm_alloc(es)
            FG = P.sbuf("FG", [128, 8], F32, es)
            r_FG = P.res()
            P.dma(FG[:], self.fing, writes=[r_FG])
            OT = [P.sbuf(f"OT{i}", [128, 8, 512], F32, es) for i in range(2)]
            r_OT = [P.res(), P.res()]
            for ti in range(4):
                a, b = ti * 512, (ti + 1) * 512
                n = 512
                i = ti % 2
                SQ, RS = self.SQ, self.RS[i]
                P.act(SQ[:, :, :n], R[:, :, a:b], AF.Square, reads=[r_R], writes=[self.r_SQ])
                psb = 6 + i
                ps = self.bank(psb)
                for kc in range(8):
                    P.mm(ps[:, :n], self.ONESb[:], SQ[:, kc, :n], kc == 0, kc == 7,
                         reads=[self.r_SQ, self.r_const], writes=[self.r_ps[psb]])
                P.act(RS[:, :n], ps[:, :n], AF.Sqrt, reads=[self.r_ps[psb]], writes=[self.r_RS[i]],
                      scale=1.0 / D, bias=self.eps_ap(EPS))
                P.recip(RS[:, :n], RS[:, :n], reads=[self.r_RS[i]], writes=[self.r_RS[i]])
                for kc in range(8):
                    P.stt(OT[i][:, kc, :], R[:, kc, a:b], FG[:, kc:kc + 1], RS[:, :n], ALU.mult, ALU.mult,
                          reads=[r_R, self.r_RS[i], r_FG], writes=[r_OT[i]])
                d = P.dma(self.outT[bi][:, :, a:b], OT[i][:], reads=[r_OT[i]])
                self.out_dmas.append(d)
            P.barrier()

    def build(self):
        P = self.P
        st = self.stages
        self.cast_weights()
        self.phase_mod()
        for bi in range(self.nb):
            P.dma(self.Rl[:], self.xT[bi], writes=[self.r_Rl])
            P.dma(self.Rc[:], self.cT[bi], writes=[self.r_Rc])
            if "ab" in st:
                self.ab(bi)
            if "ffn0" in st:
                self.ffn(0, bi, "l")
                self.ffn(0, self.nb, "c")
            if "mla" in st:
                self.mla(bi)
            if "ffn1" in st:
                self.ffn(1, bi, "l")
            self.final(bi) if "final" in st else self.dump(bi)
        P.emit(self.out_dmas)
        return self.nc

    def dump(self, bi):
        P = self.P
        for kc in range(8):
            d = P.dma(self.outT[bi][:, kc, :], self.Rl[:, kc, :], reads=[self.r_Rl])
            self.out_dmas.append(d)
        P.barrier()


def fm(v, n=None):
    v = np.asarray(v, np.float32)
    lead = v.shape[:-1]
    k = v.shape[-1] // 128
    v = v.reshape(*lead, k, 128)
    return np.ascontiguousarray(np.moveaxis(v, -1, 0))


def prep_shared(I):
    S = {}
    S["ada_w"] = np.ascontiguousarray(I["ada_w"], np.float32)
    S["ada_bT"] = fm(I["ada_b"])
    S["n1g"] = fm(I["norm1_g"])
    S["n2g"] = fm(I["norm2_g"])
    S["fing"] = fm(I["final_g"])
    S["ffn_w_up"] = np.ascontiguousarray(I["ffn_w_up"], np.float32)
    S["ffn_w_down"] = np.ascontiguousarray(I["ffn_w_down"], np.float32)
    cw = np.asarray(I["ffn_conv_w"], np.float32)
    S["ffn_cw"] = np.ascontiguousarray(cw.reshape(2, 3, 44, 128).transpose(3, 0, 2, 1))
    S["a_win"] = np.ascontiguousarray(I["ab_w_in"][0], np.float32)
    S["a_wout"] = np.ascontiguousarray(I["ab_w_out"][0], np.float32)
    S["a_lng"] = np.ascontiguousarray(np.broadcast_to(np.asarray(I["a_ln_g"][0], np.float32)[None], (128, 512)))
    S["a_lnb"] = np.ascontiguousarray(np.broadcast_to(np.asarray(I["a_ln_b"][0], np.float32)[None], (128, 512)))
    S["a_wsT"] = np.ascontiguousarray(np.asarray(I["a_ws"][0], np.float32).transpose(2, 0, 1))
    S["a_bsb"] = np.ascontiguousarray(np.broadcast_to(np.asarray(I["a_bs"][0], np.float32)[None], (128, 4, 128)))
    bcw = np.asarray(I["b_conv_w"][0], np.float32)
    S["b_cw"] = np.ascontiguousarray(bcw.reshape(3, 12, 128).transpose(2, 1, 0))
    S["b_alog"] = np.ascontiguousarray(np.broadcast_to(np.asarray(I["b_a_log"][0], np.float32).reshape(1, 8), (128, 8)))
    S["b_dtb"] = np.ascontiguousarray(np.broadcast_to(np.asarray(I["b_dt_bias"][0], np.float32).reshape(1, 8), (128, 8)))
    S["b_ng"] = np.ascontiguousarray(np.asarray(I["b_norm_g"][0], np.float32).reshape(128, 1))
    S["ident"] = np.eye(128, dtype=np.float32)
    jj, ii = np.meshgrid(np.arange(64), np.arange(64), indexing="ij")
    S["maskc"] = np.ascontiguousarray(np.stack([(jj <= ii), (jj >= ii)], 1).astype(np.float32))
    S["strict"] = np.ascontiguousarray(np.stack([(jj < ii), (jj > ii)], 1).astype(np.float32))
    w_in = np.asarray(I["mla_w_in"][0], np.float32)
    perm = (np.arange(64) + 32) % 64
    S["m_win"] = np.ascontiguousarray(np.concatenate([w_in, w_in[:, 640 + perm]], 1))
    wuq = np.asarray(I["mla_w_uq"][0], np.float32).reshape(384, 8, 192)
    S["m_wuq"] = np.ascontiguousarray(np.concatenate([wuq, wuq[:, :, 128 + perm]], 2).reshape(384, 2048))
    wukv = np.asarray(I["mla_w_ukv"][0], np.float32).reshape(256, 8, 256)
    S["m_wukv"] = np.ascontiguousarray(np.concatenate([wukv[:, :, :128].reshape(256, 1024),
                                                       wukv[:, :, 128:].reshape(256, 1024)], 1))
    S["m_wout"] = np.ascontiguousarray(I["mla_w_out"][0], np.float32)
    S["m_qng"] = fm(I["mla_q_norm_g"][0])
    S["m_kvng"] = fm(I["mla_kv_norm_g"][0])
    pos = np.arange(SEQ)
    row = (pos // 64).astype(np.float32)
    col = (pos % 64).astype(np.float32)
    inv = (np.float32(10000.0) ** (-np.arange(16, dtype=np.float32) / np.float32(16))).astype(np.float32)
    ang = np.concatenate([row[:, None] * inv, col[:, None] * inv], -1).astype(np.float32)
    cs, sn = np.cos(ang).T.astype(np.float32), np.sin(ang).T.astype(np.float32)
    S["ropeC"] = np.ascontiguousarray(np.concatenate([cs, cs], 0))
    S["ropeS"] = np.ascontiguousarray(np.concatenate([-sn, sn], 0))
    return S


def prep_core(I, b0, nb):
    x = np.asarray(I["x"][b0:b0 + nb], np.float32)
    ctx = np.asarray(I["ctx"][b0:b0 + nb], np.float32)
    C = {}
    C["xT"] = np.ascontiguousarray(x.reshape(nb, SEQ, 8, 128).transpose(0, 3, 2, 1))
    C["cT"] = np.ascontiguousarray(ctx.reshape(nb, CTX, 8, 128).transpose(0, 3, 2, 1))
    cc = np.concatenate([np.asarray(I["c"][b0:b0 + nb], np.float32), np.asarray(I["c_ctx"], np.float32)[None]], 0)
    C["ccT"] = np.ascontiguousarray(cc.reshape(nb + 1, 8, 128).transpose(2, 1, 0))
    return C


_CACHE = {}


def run(I, nb, ncores, stages, trace=False):
    key = (nb, tuple(stages))
    if key not in _CACHE:
        _CACHE[key] = Builder(nb, stages)
        _CACHE[key].build()
    B = _CACHE[key]
    S = prep_shared(I)
    in_maps = []
    for c in range(ncores):
        m = dict(S)
        m.update(prep_core(I, c * nb, nb))
        in_maps.append({k: m[k] for k in B.inputs})
    res = run_bass_kernel_spmd(B.nc, in_maps, core_ids=list(range(ncores)), trace=trace)
    outs = []
    for r in res.results:
        o = r["outT"]
        outs.append(o.transpose(0, 3, 2, 1).reshape(nb, SEQ, D))
    return np.concatenate(outs, 0), res


ALL_STAGES = ("ab", "ffn0", "mla", "ffn1", "final")


def kernel(**inputs):
    out, _ = run(inputs, 4, NCORES, ALL_STAGES)
    return np.ascontiguousarray(out.astype(np.float32))
```

```python
from contextlib import ExitStack
import numpy as np
import concourse.bass as bass
import concourse.mybir as mybir
from concourse.bass_utils import run_bass_kernel_spmd

F32 = mybir.dt.float32
BF16 = mybir.dt.bfloat16
AF = mybir.ActivationFunctionType
ALU = mybir.AluOpType
AX = mybir.AxisListType

EPOCH = 16000
RING = 12
SAME_ENGINE_RAW = True
INV_BF16 = False
EPS = 1e-6
NCORES = 8
D = 1024
SEQ = 2048
CTX = 256
DFF = 2816


class Res:
    __slots__ = ("name", "last_w", "readers")

    def __init__(self, name):
        self.name = name
        self.last_w = None
        self.readers = {}


class Op:
    __slots__ = ("eng", "fn", "deps", "signal", "is_dma", "sig", "pos", "ring_wait", "d2d")

    def __init__(self, eng, fn, is_dma):
        self.eng = eng
        self.fn = fn
        self.deps = {}
        self.signal = False
        self.is_dma = is_dma
        self.sig = None
        self.ring_wait = None
        self.d2d = False


class Prog:
    STREAMS = ("pe", "act", "dve", "pool", "sp")

    def __init__(self, nc):
        self.nc = nc
        self.ops = {s: [] for s in self.STREAMS}
        self.es = ExitStack()
        self.n = 0
        self.uid = 0

    def sbuf(self, name, shape, dt, es=None):
        self.uid += 1
        return (es or self.es).enter_context(self.nc.sbuf_tensor(f"{name}_{self.uid}", list(shape), dt))

    def psum(self, name, shape, dt=F32):
        return self.es.enter_context(self.nc.psum_tensor(name, list(shape), dt))

    def res(self, name="r"):
        return Res(name)

    def op(self, eng, fn, reads=(), writes=(), dma=False):
        o = Op(eng, fn, dma)
        o.pos = self.n
        self.n += 1
        for r in reads:
            if r.last_w is not None:
                self._dep(o, r.last_w, True)
        for r in writes:
            if r.last_w is not None:
                self._dep(o, r.last_w, False)
            for rd in r.readers.values():
                self._dep(o, rd, False)
        for r in reads:
            r.readers[(o.eng, o.pos) if dma else o.eng] = o
        for r in writes:
            r.last_w = o
            r.readers = {}
        self.ops[eng].append(o)
        return o

    def _dep(self, o, src, raw):
        if src is o:
            return
        if not src.is_dma and not o.is_dma and src.eng == o.eng:
            if o.eng == "pe" or not (raw and SAME_ENGINE_RAW):
                return
        key = (src.eng, src.pos) if src.is_dma else src.eng
        prev = o.deps.get(key)
        if prev is None or prev.pos < src.pos:
            o.deps[key] = src
        src.signal = True

    def barrier(self):
        lasts = []
        for s in self.STREAMS:
            ol = self.ops[s]
            if not ol:
                continue
            for o in reversed(ol):
                if o.fn is not None and not o.is_dma:
                    lasts.append(o)
                    break
            nd = 0
            for o in reversed(ol):
                if o.is_dma and not o.d2d:
                    lasts.append(o)
                    nd += 1
                    if nd >= RING:
                        break
        for s in self.STREAMS:
            o = Op(s, None, False)
            o.pos = self.n
            self.n += 1
            for src in lasts:
                if src.eng == s and not src.is_dma:
                    continue
                key = (src.eng, src.pos) if src.is_dma else src.eng
                o.deps[key] = src
                src.signal = True
            self.ops[s].append(o)

    def dma(self, out, in_, reads=(), writes=(), eng="sp", d2d=False, **kw):
        o = self.op(eng, lambda e: e.dma_start(out=out, in_=in_, **kw), reads, writes, dma=True)
        o.d2d = d2d
        return o

    def mm(self, out, lhsT, rhs, start, stop, reads=(), writes=(), **kw):
        return self.op("pe", lambda e: e.matmul(out, lhsT, rhs, start=start, stop=stop, **kw), reads, writes)

    def act(self, out, in_, func, reads=(), writes=(), **kw):
        return self.op("act", lambda e: e.activation(out=out, in_=in_, func=func, **kw), reads, writes)

    def dve(self, fn, reads=(), writes=()):
        return self.op("dve", fn, reads, writes)

    def stt(self, out, in0, scalar, in1, op0, op1, reads=(), writes=(), eng="dve"):
        return self.op(eng, lambda e: e.scalar_tensor_tensor(out, in0, scalar, in1, op0, op1), reads, writes)

    def ts(self, out, in0, s1, s2, op0, op1=None, reads=(), writes=(), eng="dve"):
        if op1 is None:
            return self.op(eng, lambda e: e.tensor_scalar(out, in0, s1, None, op0), reads, writes)
        return self.op(eng, lambda e: e.tensor_scalar(out, in0, s1, s2, op0, op1), reads, writes)

    def tt(self, out, in0, in1, op, reads=(), writes=(), eng="dve"):
        return self.op(eng, lambda e: e.tensor_tensor(out, in0, in1, op), reads, writes)

    def recip(self, out, in_, reads=(), writes=()):
        return self.op("dve", lambda e: e.reciprocal(out, in_), reads, writes)

    def copy(self, out, in_, reads=(), writes=(), eng="dve"):
        if eng == "act":
            return self.op(eng, lambda e: e.activation(out=out, in_=in_, func=AF.Copy), reads, writes)
        return self.op(eng, lambda e: e.tensor_copy(out, in_), reads, writes)

    def memset(self, ap, v, writes=(), eng="pool"):
        return self.op(eng, lambda e: e.memset(ap, v), (), writes)

    def emit(self, final_wait_ops=()):
        nc = self.nc
        sems = {}

        def get_sem(name):
            if name not in sems:
                sems[name] = self.es.enter_context(nc.semaphore(name))
            return sems[name]

        for s in self.STREAMS:
            cnt = 0
            nd = 0
            hist = []
            for o in self.ops[s]:
                if o.is_dma:
                    slot = nd % RING
                    o.sig = (get_sem(f"d_{s}_{slot}"), 16 * (nd // RING + 1))
                    if nd >= RING:
                        o.ring_wait = hist[nd - RING].sig
                    hist.append(o)
                    nd += 1
                elif o.signal:
                    cnt += 1
                    ep = (cnt - 1) // EPOCH
                    o.sig = (get_sem(f"c_{s}_{ep}"), cnt - ep * EPOCH)
        final_sigs = [o.sig for o in final_wait_ops]
        with nc.Block() as block:
            def make(s):
                def body(e):
                    waited = {}

                    def wait(sig):
                        sem, val = sig
                        k = id(sem)
                        if waited.get(k, 0) >= val:
                            return
                        waited[k] = val
                        e.wait_ge(sem, val)

                    for o in self.ops[s]:
                        if o.ring_wait is not None:
                            wait(o.ring_wait)
                        for d in o.deps.values():
                            wait(d.sig)
                        if o.fn is None:
                            continue
                        ins = o.fn(e)
                        if o.is_dma:
                            ins.then_inc(o.sig[0], 16)
                        elif o.signal:
                            ins.then_inc(o.sig[0], 1)
                    if s == "sp":
                        for sg in final_sigs:
                            wait(sg)
                return body

            block.tensor(make("pe"))
            block.scalar(make("act"))
            block.vector(make("dve"))
            block.gpsimd(make("pool"))
            block.sync(make("sp"))
        self.es.close()


class Builder:
    def __init__(self, nb, stages, dbg=()):
        self.nb = nb
        self.stages = stages
        self.dbg = dbg
        nc = self.nc = bass.Bass("TRN2", target_bir_lowering=False)
        self.P = P = Prog(nc)
        self.inputs = {}
        nj = nb + 1
        self.nj = nj

        def inp(name, shape, dt=F32):
            t = nc.dram_tensor(name, list(shape), dt, kind="ExternalInput").ap()
            self.inputs[name] = t
            return t

        def scratch(name, shape, dt=BF16):
            return nc.dram_tensor(name, list(shape), dt, kind="Internal").ap()

        self.xT = inp("xT", [nb, 128, 8, SEQ])
        self.cT = inp("cT", [nb, 128, 8, CTX])
        self.ccT = inp("ccT", [128, 8, nj])
        self.ada_w = inp("ada_w", [2, D, 6 * D])
        self.ada_bT = inp("ada_bT", [128, 2, 48])
        self.n1g = inp("n1g", [128, 2, 8])
        self.n2g = inp("n2g", [128, 2, 8])
        self.fing = inp("fing", [128, 8])
        self.w_up = inp("ffn_w_up", [2, D, 2 * DFF])
        self.w_down = inp("ffn_w_down", [2, DFF, D])
        self.ffn_cw = inp("ffn_cw", [128, 2, 44, 3])
        self.a_win = inp("a_win", [D, 3088])
        self.a_wout = inp("a_wout", [D, D])
        self.a_lng = inp("a_lng", [128, 512])
        self.a_lnb = inp("a_lnb", [128, 512])
        self.a_wsT = inp("a_wsT", [128, 4, 128])
        self.a_bsb = inp("a_bsb", [128, 4, 128])
        self.b_cw = inp("b_cw", [128, 12, 3])
        self.b_alog = inp("b_alog", [128, 8])
        self.b_dtb = inp("b_dtb", [128, 8])
        self.b_ng = inp("b_ng", [128, 1])
        self.ident = inp("ident", [128, 128])
        self.maskc = inp("maskc", [64, 2, 64])
        self.strict = inp("strict", [64, 2, 64])
        self.WINA = scratch("WINA", [128, 8, 3088])
        self.WOA = scratch("WOA", [128, 8, D])
        self.r_WA = P.res("WA")
        self.QKV = scratch("QKV", [CTX + SEQ, 1552], F32)
        self.GATE = scratch("GATE", [128, 4, CTX + SEQ])
        self.YA = scratch("YA", [128, 4, CTX + SEQ])
        self.r_QKV, self.r_GATE, self.r_YA = P.res(), P.res(), P.res()
        self.m_win = inp("m_win", [D, 768])
        self.m_wuq = inp("m_wuq", [384, 2048])
        self.m_wukv = inp("m_wukv", [256, 2048])
        self.m_wout = inp("m_wout", [D, D])
        self.m_qng = inp("m_qng", [128, 3])
        self.m_kvng = inp("m_kvng", [128, 2])
        self.ropeC = inp("ropeC", [64, SEQ])
        self.ropeS = inp("ropeS", [64, SEQ])
        self.WIN1 = scratch("WIN1", [128, 8, 768])
        self.WUQ = scratch("WUQ", [128, 3, 2048])
        self.WUKV = scratch("WUKV", [128, 2, 2048])
        self.WO1 = scratch("WO1", [128, 8, D])
        self.r_W1 = P.res("W1")
        self.outT = nc.dram_tensor("outT", [nb, 128, 8, SEQ], F32, kind="ExternalOutput").ap()
        self.dbg_out = {}
        self.WU = scratch("WU", [2, 22, 128, 8, 2, 128])
        self.WD = scratch("WD", [2, 128, 22, D])
        self.r_WU = [P.res("WU0"), P.res("WU1")]
        self.r_WD = [P.res("WD0"), P.res("WD1")]
        self.Rl = P.sbuf("Rl", [128, 8, SEQ], F32)
        self.Rc = P.sbuf("Rc", [128, 8, CTX], F32)
        self.r_Rl = P.res("Rl")
        self.r_Rc = P.res("Rc")
        self.MOD = P.sbuf("MOD", [128, 2, 48, nj], F32)
        self.GS = P.sbuf("GS", [128, 2, 2, 8, nj], F32)
        self.r_MOD = P.res("MOD")
        self.ONESb = P.sbuf("ONESb", [128, 128], BF16)
        self.EPSC = P.sbuf("EPSC", [128, 4], F32)
        self.r_const = P.res("const")
        self.PS = P.psum("PS", [128, 4096], F32)
        self.r_ps = [P.res(f"ps{i}") for i in range(8)]
        self.out_dmas = []

    def bank(self, i):
        return self.PS[:, 512 * i:512 * (i + 1)]

    def cast_weights(self):
        P = self.P

        def ffn(l):
            src = self.w_up[l].rearrange("(kc p) (gu c j) -> c p kc gu j", p=128, gu=2, j=128)
            for c in range(22):
                for gu in range(2):
                    P.dma(self.WU[l, c][:, :, gu, :], src[c][:, :, gu, :], writes=[self.r_WU[l]], eng="pool", d2d=True)
            srcd = self.w_down[l].rearrange("(c p) o -> p c o", p=128)
            for c0 in range(0, 22, 2):
                P.dma(self.WD[l][:, c0:c0 + 2, :], srcd[:, c0:c0 + 2, :], writes=[self.r_WD[l]], eng="pool", d2d=True)

        wsrc = self.a_win.rearrange("(kc p) n -> p kc n", p=128)
        for c0 in range(0, 3088, 772):
            P.dma(self.WINA[:, :, c0:c0 + 772], wsrc[:, :, c0:c0 + 772], writes=[self.r_WA], eng="pool", d2d=True)
        P.dma(self.WOA, self.a_wout.rearrange("(kc p) n -> p kc n", p=128), writes=[self.r_WA], eng="pool", d2d=True)
        ffn(0)
        P.dma(self.WIN1, self.m_win.rearrange("(kc p) n -> p kc n", p=128), writes=[self.r_W1], eng="pool", d2d=True)
        P.dma(self.WUQ, self.m_wuq.rearrange("(kc p) n -> p kc n", p=128), writes=[self.r_W1], eng="pool", d2d=True)
        P.dma(self.WUKV, self.m_wukv.rearrange("(kc p) n -> p kc n", p=128), writes=[self.r_W1], eng="pool", d2d=True)
        P.dma(self.WO1, self.m_wout.rearrange("(kc p) n -> p kc n", p=128), writes=[self.r_W1], eng="pool", d2d=True)
        ffn(1)

    def phase_mod(self):
        P = self.P
        nj = self.nj
        with ExitStack() as es:
            cc = P.sbuf("cc", [128, 8, nj], F32, es)
            sc = P.sbuf("sc", [128, 8, nj], F32, es)
            bT = P.sbuf("bT", [128, 2, 48], F32, es)
            g1 = P.sbuf("g1", [128, 2, 8], F32, es)
            g2 = P.sbuf("g2", [128, 2, 8], F32, es)
            wt = [P.sbuf(f"adaw{i}", [128, 8, 512], F32, es) for i in range(2)]
            r_w = [P.res(), P.res()]
            r_cc, r_sc, r_bT, r_g = P.res(), P.res(), P.res(), P.res()
            P.dma(cc[:], self.ccT, writes=[r_cc])
            P.dma(bT[:], self.ada_bT, writes=[r_bT])
            P.dma(g1[:], self.n1g, writes=[r_g])
            P.dma(g2[:], self.n2g, writes=[r_g])
            P.memset(self.ONESb[:], 1.0, writes=[self.r_const], eng="dve")
            P.memset(self.EPSC[:, 0:1], EPS, writes=[self.r_const], eng="dve")
            P.memset(self.EPSC[:, 3:4], 1.0, writes=[self.r_const], eng="dve")
            P.act(sc[:], cc[:], AF.Silu, reads=[r_cc], writes=[r_sc])
            blk = 0
            for l in range(2):
                wsrc = self.ada_w[l].rearrange("(kc p) n -> p kc n", p=128)
                for nb_ in range(12):
                    s = blk % 2
                    P.dma(wt[s][:], wsrc[:, :, nb_ * 512:(nb_ + 1) * 512], writes=[r_w[s]])
                    pb = blk % 2
                    ps = self.bank(pb)
                    for mi in range(4):
                        for kc in range(8):
                            P.mm(ps[:, mi * nj:(mi + 1) * nj], wt[s][:, kc, mi * 128:(mi + 1) * 128], sc[:, kc, :],
                                 kc == 0, kc == 7, reads=[r_w[s], r_sc], writes=[self.r_ps[pb]])
                    for mi in range(4):
                        m = nb_ * 4 + mi
                        P.ts(self.MOD[:, l, m, :], ps[:, mi * nj:(mi + 1) * nj], bT[:, l, m:m + 1], None, ALU.add,
                             reads=[self.r_ps[pb], r_bT], writes=[self.r_MOD])
                    blk += 1
            for l in range(2):
                for t, (gt, v) in enumerate(((g1, 1), (g2, 4))):
                    for kc in range(8):
                        P.ts(self.GS[:, l, t, kc, :], self.MOD[:, l, v * 8 + kc, :], 1.0, gt[:, l, kc:kc + 1],
                             ALU.add, ALU.mult, reads=[self.r_MOD, r_g], writes=[self.r_MOD])
            P.barrier()

    def norm_alloc(self, es):
        P = self.P
        self.SQ = P.sbuf("SQ", [128, 8, 512], BF16, es)
        self.RS = [P.sbuf(f"RS{i}", [128, 512], F32, es) for i in range(2)]
        self.NT = [P.sbuf(f"NT{i}", [128, 512], F32, es) for i in range(2)]
        self.r_SQ = P.res()
        self.r_RS = [P.res(), P.res()]
        self.r_NT = [P.res(), P.res()]
        self.norm_i = 0
        self.nt_i = 0

    def norm_tile(self, R, r_R, a, b, HT, r_HT, off, l, t, j, psb):
        P = self.P
        n = b - a
        i = self.norm_i % 2
        self.norm_i += 1
        SQ, RS = self.SQ, self.RS[i]
        P.act(SQ[:, :, :n], R[:, :, a:b], AF.Square, reads=[r_R], writes=[self.r_SQ])
        ps = self.bank(psb)
        for kc in range(8):
            P.mm(ps[:, :n], self.ONESb[:], SQ[:, kc, :n], kc == 0, kc == 7,
                 reads=[self.r_SQ, self.r_const], writes=[self.r_ps[psb]])
        P.act(RS[:, :n], ps[:, :n], AF.Sqrt, reads=[self.r_ps[psb]], writes=[self.r_RS[i]],
              scale=1.0 / D, bias=self.eps_ap(EPS))
        P.recip(RS[:, :n], RS[:, :n], reads=[self.r_RS[i]], writes=[self.r_RS[i]])
        shv = 0 if t == 0 else 3
        for kc in range(8):
            k = self.nt_i % 2
            self.nt_i += 1
            NT = self.NT[k]
            P.stt(NT[:, :n], R[:, kc, a:b], self.GS[:, l, t, kc, j:j + 1], RS[:, :n], ALU.mult, ALU.mult,
                  reads=[r_R, self.r_RS[i], self.r_MOD], writes=[self.r_NT[k]])
            P.act(HT[:, kc, off:off + n], NT[:, :n], AF.Identity, reads=[self.r_NT[k], self.r_MOD], writes=[r_HT],
                  bias=self.MOD[:, l, shv * 8 + kc, j:j + 1], scale=1.0)

    def eps_ap(self, v):
        return self.EPSC[:, 0:1]

    def ffn(self, l, j, which):
        P = self.P
        R, r_R, L = (self.Rl, self.r_Rl, SEQ) if which == "l" else (self.Rc, self.r_Rc, CTX)
        with ExitStack() as es:
            self.norm_alloc(es)
            WDs = P.sbuf("WDs", [128, 22, D], BF16, es)
            r_WDs = P.res()
            WUs = [P.sbuf(f"WUs{i}", [128, 8, 2, 128], BF16, es) for i in range(3)]
            r_WUs = [P.res() for _ in range(3)]
            HT = [P.sbuf(f"HTf{i}", [128, 8, 512], BF16, es) for i in range(2)]
            r_HT = [P.res(), P.res()]
            WMAX = 410
            Hs = [P.sbuf(f"H{i}", [128, 22, WMAX], BF16, es) for i in range(2)]
            r_Hs = [P.res(), P.res()]
            A1 = [[P.sbuf(f"A1_{i}{g}", [128, WMAX], F32, es) for g in range(2)] for i in range(2)]
            r_A1 = [[P.res(), P.res()] for _ in range(2)]
            CW = P.sbuf("CW", [128, 44, 3], F32, es)
            r_CW = P.res()
            P.dma(CW[:], self.ffn_cw[:, l], writes=[r_CW])
            for c0 in range(0, 22, 2):
                P.dma(WDs[:, c0:c0 + 2, :], self.WD[l][:, c0:c0 + 2, :], reads=[self.r_WD[l]], writes=[r_WDs])
            wins = []
            t = 0
            while t < L:
                n = min(WMAX, L - t)
                wins.append((t, n))
                t += n
            pair_i = 0
            g2v = 5

            def down(o, wi):
                t0, n_out = wins[wi]
                H, r_H = Hs[wi % 2], r_Hs[wi % 2]
                pb = 6 + (o % 2)
                ps = self.bank(pb)
                for c in range(22):
                    P.mm(ps[:, :n_out], WDs[:, c, o * 128:(o + 1) * 128], H[:, c, :n_out], c == 0, c == 21,
                         reads=[r_WDs, r_H], writes=[self.r_ps[pb]])
                P.stt(R[:, o, t0:t0 + n_out], ps[:, :n_out], self.MOD[:, l, g2v * 8 + o, j:j + 1],
                      R[:, o, t0:t0 + n_out], ALU.mult, ALU.add,
                      reads=[self.r_ps[pb], r_R, self.r_MOD], writes=[r_R])

            for wi, (t0, n_out) in enumerate(wins):
                H, r_H = Hs[wi % 2], r_Hs[wi % 2]
                n_in = n_out + 2
                ht, r_ht = HT[wi % 2], r_HT[wi % 2]
                a = t0
                b = min(t0 + n_out + 1, L)
                off = 1
                if t0 == 0:
                    P.memset(ht[:, :, 0:1], 0.0, writes=[r_ht])
                else:
                    pht, r_pht = HT[(wi - 1) % 2], r_HT[(wi - 1) % 2]
                    pn = wins[wi - 1][1]
                    P.copy(ht[:, :, 0:1], pht[:, :, pn:pn + 1], reads=[r_pht], writes=[r_ht], eng="pool")
                if t0 + n_out == L:
                    P.memset(ht[:, :, n_in - 1:n_in], 0.0, writes=[r_ht])
                self.norm_tile(R, r_R, a, b, ht, r_ht, off, l, 1, j, 6)
                for c in range(22):
                    s = pair_i % 3
                    P.dma(WUs[s][:], self.WU[l, c], reads=[self.r_WU[l]], writes=[r_WUs[s]])
                    pb = 2 * (pair_i % 3)
                    ai = pair_i % 2
                    pair_i += 1
                    for gu in range(2):
                        ps = self.bank(pb + gu)
                        for kc in range(8):
                            P.mm(ps[:, :n_in], WUs[s][:, kc, gu, :], ht[:, kc, :n_in], kc == 0, kc == 7,
                                 reads=[r_WUs[s], r_ht], writes=[self.r_ps[pb + gu]])
                    for gu in range(2):
                        ps = self.bank(pb + gu)
                        ch = c + 22 * gu
                        A = A1[ai][gu]
                        rA = r_A1[ai][gu]
                        rp = self.r_ps[pb + gu]
                        P.act(A[:, :n_out], ps[:, 1:1 + n_out], AF.Identity, reads=[rp, r_CW], writes=[rA],
                              scale=CW[:, ch, 1:2])
                        P.stt(A[:, :n_out], ps[:, 0:n_out], CW[:, ch, 0:1], A[:, :n_out], ALU.mult, ALU.add,
                              reads=[rp, rA, r_CW], writes=[rA])
                        P.stt(A[:, :n_out], ps[:, 2:2 + n_out], CW[:, ch, 2:3], A[:, :n_out], ALU.mult, ALU.add,
                              reads=[rp, rA, r_CW], writes=[rA])
                    Ag, Au = A1[ai]
                    P.act(Ag[:, :n_out], Ag[:, :n_out], AF.Silu, reads=[r_A1[ai][0]], writes=[r_A1[ai][0]])
                    P.tt(H[:, c, :n_out], Ag[:, :n_out], Au[:, :n_out], ALU.mult,
                         reads=[r_A1[ai][0], r_A1[ai][1]], writes=[r_H], eng="pool")
                    if wi >= 1 and c >= 4 and c % 2 == 0 and (c - 4) // 2 < 8:
                        down((c - 4) // 2, wi - 1)
            for o in range(8):
                down(o, len(wins) - 1)
            P.barrier()

    def ap(self, T, off, dims):
        rowlen = 1
        for d in T.shape[1:]:
            rowlen *= d
        return bass.AP(T, off, [[rowlen, dims[0]]] + [list(d) for d in dims[1:]])

    def ab_proj(self, bi):
        P = self.P
        l = 0
        with ExitStack() as es:
            self.norm_alloc(es)
            WIN = P.sbuf("WINa", [128, 8, 3088], BF16, es)
            r_WIN = P.res()
            for c0 in range(0, 3088, 772):
                P.dma(WIN[:, :, c0:c0 + 772], self.WINA[:, :, c0:c0 + 772], reads=[self.r_WA], writes=[r_WIN])
            LNG = P.sbuf("LNG", [128, 512], F32, es)
            LNB = P.sbuf("LNB", [128, 512], F32, es)
            WST = P.sbuf("WST", [128, 4, 128], BF16, es)
            BSB = P.sbuf("BSB", [128, 4, 128], F32, es)
            BCW = P.sbuf("BCW", [128, 12, 3], F32, es)
            ALG = P.sbuf("ALG", [128, 8], F32, es)
            DTB = P.sbuf("DTB", [128, 8], F32, es)
            IDN = P.sbuf("IDN", [128, 128], F32, es)
            r_c = P.res()
            P.dma(LNG[:], self.a_lng, writes=[r_c])
            P.dma(LNB[:], self.a_lnb, writes=[r_c])
            P.dma(WST[:], self.a_wsT, writes=[r_c], eng="pool")
            P.dma(BSB[:], self.a_bsb, writes=[r_c])
            P.dma(BCW[:], self.b_cw, writes=[r_c])
            P.dma(ALG[:], self.b_alog, writes=[r_c])
            P.dma(DTB[:], self.b_dtb, writes=[r_c])
            P.dma(IDN[:], self.ident, writes=[r_c])
            P.act(ALG[:], ALG[:], AF.Exp, reads=[r_c], writes=[r_c])
            P.ts(ALG[:], ALG[:], -1.0, None, ALU.mult, reads=[r_c], writes=[r_c])
            HT = P.sbuf("HTa", [128, 8, 386], BF16, es)
            r_HT = P.res()
            AU = P.sbuf("AU", [128, 4, 384], BF16, es)
            r_AU = P.res()
            GT = P.sbuf("GTs", [128, 4, 384], BF16, es)
            r_GT = P.res()
            YAs = P.sbuf("YAs", [128, 4, 384], BF16, es)
            r_YAs = P.res()
            CV = [P.sbuf(f"CV{i}", [128, 384], F32, es) for i in range(2)]
            r_CV = [P.res(), P.res()]
            ST = [P.sbuf(f"ST{i}", [128, 1552], F32, es) for i in range(3)]
            r_ST = [P.res() for _ in range(3)]
            r_STg = [P.res() for _ in range(3)]
            r_t3 = P.res()
            SQT = P.sbuf("SQT", [128, 1024], F32, es)
            SS = P.sbuf("SSt", [128, 16], F32, es)
            r_t = P.res()
            GV = P.sbuf("GV", [128, 512], F32, es)
            GV2 = P.sbuf("GV2", [128, 512], F32, es)
            VN = P.sbuf("VNb", [128, 512], BF16, es)
            r_GV, r_VN = P.res(), P.res()
            TMPY = P.sbuf("TMPY", [128, 512], F32, es)
            r_TMPY = P.res()
            AB8 = P.sbuf("AB8", [128, 16], F32, es)
            r_AB8 = P.res()
            wins = [("c", 0, CTX)] + [("l", t0, min(384, SEQ - t0)) for t0 in range(0, SEQ, 384)]
            for which, t0, n_out in wins:
                R, r_R, L, jj, g0 = (self.Rl, self.r_Rl, SEQ, bi, CTX) if which == "l" else (self.Rc, self.r_Rc, CTX, self.nb, 0)
                n_in = n_out + 2
                a = max(t0 - 1, 0)
                b = min(t0 + n_out + 1, L)
                off = a - (t0 - 1)
                if t0 == 0:
                    P.memset(HT[:, :, 0:1], 0.0, writes=[r_HT], eng="dve")
                if t0 + n_out == L:
                    P.memset(HT[:, :, n_in - 1:n_in], 0.0, writes=[r_HT], eng="dve")
                self.norm_tile(R, r_R, a, b, HT, r_HT, off, l, 0, jj, 7)
                tokg = g0 + t0
                ntb = n_out // 128
                for g in range(4):
                    for kind, col0, func, DST, r_DST in ((0, g * 128, AF.Gelu, AU, r_AU), (1, 2560 + g * 128, AF.Silu, GT, r_GT)):
                        pb = (2 * g + kind) % 3
                        ps = self.bank(pb)
                        for kc in range(8):
                            P.mm(ps[:, :n_out], WIN[:, kc, col0:col0 + 128], HT[:, kc, 1:1 + n_out], kc == 0, kc == 7,
                                 reads=[r_WIN, r_HT], writes=[self.r_ps[pb]])
                        P.act(DST[:, g, :n_out], ps[:, :n_out], func, reads=[self.r_ps[pb]], writes=[r_DST])
                P.dma(self.GATE[:, :, tokg:tokg + n_out], GT[:, :, :n_out], reads=[r_GT], writes=[self.r_GATE])
                def transposes(ci, cv, r_cv):
                    kind, hh = ci // 4, ci % 4
                    for tb in range(ntb):
                        pt = self.bank(3 + tb)
                        P.mm(pt[:, hh * 128:(hh + 1) * 128], cv[:, tb * 128:(tb + 1) * 128], IDN[:], True, True,
                             reads=[r_cv, r_c], writes=[self.r_ps[3 + tb]])
                    if hh == 3:
                        for tb in range(ntb):
                            pt = self.bank(3 + tb)
                            P.copy(ST[tb][:, kind * 512:(kind + 1) * 512], pt[:, :], reads=[self.r_ps[3 + tb]],
                                   writes=[r_ST[tb]], eng="act" if tb == 1 else "dve")
                prev = None
                for ci in range(12):
                    pb = ci % 3
                    ps = self.bank(pb)
                    for kc in range(8):
                        P.mm(ps[:, :n_in], WIN[:, kc, 1024 + ci * 128:1024 + (ci + 1) * 128], HT[:, kc, :n_in], kc == 0, kc == 7,
                             reads=[r_WIN, r_HT], writes=[self.r_ps[pb]])
                    if prev is not None:
                        transposes(*prev)
                    cv, r_cv = CV[ci % 2], r_CV[ci % 2]
                    rp = self.r_ps[pb]
                    P.act(cv[:, :n_out], ps[:, 1:1 + n_out], AF.Identity, reads=[rp, r_c], writes=[r_cv], scale=BCW[:, ci, 1:2])
                    P.stt(cv[:, :n_out], ps[:, 0:n_out], BCW[:, ci, 0:1], cv[:, :n_out], ALU.mult, ALU.add,
                          reads=[rp, r_cv, r_c], writes=[r_cv])
                    P.stt(cv[:, :n_out], ps[:, 2:2 + n_out], BCW[:, ci, 2:3], cv[:, :n_out], ALU.mult, ALU.add,
                          reads=[rp, r_cv, r_c], writes=[r_cv])
                    P.act(cv[:, :n_out], cv[:, :n_out], AF.Silu, reads=[r_cv], writes=[r_cv])
                    prev = (ci, cv, r_cv)
                transposes(*prev)
                def chain_l2(tb):
                    st, r_st = ST[tb], r_ST[tb]
                    P.tt(SQT[:], st[:, 0:1024], st[:, 0:1024], ALU.mult, reads=[r_st], writes=[r_t])
                    yield
                    P.op("dve", lambda e: e.tensor_reduce(SS[:, 0:8], SQT[:].rearrange("p (u d) -> p u d", u=8),
                                                          AX.X, ALU.add), reads=[r_t], writes=[r_t])
                    yield
                    P.act(SS[:, 0:8], SS[:, 0:8], AF.Sqrt, reads=[r_t, self.r_const], writes=[r_t], bias=self.EPSC[:, 0:1], scale=1.0)
                    yield
                    P.recip(SS[:, 0:8], SS[:, 0:8], reads=[r_t], writes=[r_t])
                    P.ts(SS[:, 0:4], SS[:, 0:4], 128.0 ** -0.5, None, ALU.mult, reads=[r_t], writes=[r_t])
                    yield
                    P.tt(st[:, 0:1024].rearrange("p (u d) -> p u d", u=8), st[:, 0:1024].rearrange("p (u d) -> p u d", u=8),
                         self.ap(SS, 0, [128, [1, 8], [0, 128]]), ALU.mult,
                         reads=[r_st, r_t], writes=[r_st])

                def chain_ab(tb):
                    st, r_sg = ST[tb], r_STg[tb]
                    pa = self.bank(6)
                    c1 = 1 + tb * 128
                    for kc in range(8):
                        P.mm(pa[:, 0:16], HT[:, kc, c1:c1 + 128], WIN[:, kc, 3072:3088], kc == 0, kc == 7,
                             reads=[r_WIN, r_HT], writes=[self.r_ps[6]])
                    yield
                    P.tt(AB8[:, 0:8], pa[:, 0:8], DTB[:], ALU.add, reads=[self.r_ps[6], r_c], writes=[r_AB8])
                    P.act(st[:, 1544:1552], pa[:, 8:16], AF.Sigmoid, reads=[self.r_ps[6]], writes=[r_sg])
                    yield
                    P.act(AB8[:, 0:8], AB8[:, 0:8], AF.Exp, reads=[r_AB8], writes=[r_AB8])
                    yield
                    P.act(AB8[:, 0:8], AB8[:, 0:8], AF.Ln, reads=[r_AB8, self.r_const], writes=[r_AB8], bias=self.EPSC[:, 3:4], scale=1.0)
                    yield
                    P.tt(st[:, 1536:1544], AB8[:, 0:8], ALG[:], ALU.mult, reads=[r_AB8, r_c], writes=[r_sg])

                def chain_av(tb):
                    c1 = 1 + tb * 128
                    pv = self.bank(7)
                    for kc in range(8):
                        P.mm(pv[:, :], HT[:, kc, c1:c1 + 128], WIN[:, kc, 512:1024], kc == 0, kc == 7,
                             reads=[r_WIN, r_HT], writes=[self.r_ps[7]])
                    yield
                    P.act(GV[:], pv[:, :], AF.Gelu, reads=[self.r_ps[7]], writes=[r_GV])
                    yield
                    P.op("dve", lambda e: e.tensor_reduce(SS[:, 8:9], GV[:], AX.X, ALU.add), reads=[r_GV], writes=[r_t3])
                    P.ts(SS[:, 8:9], SS[:, 8:9], 1.0 / 512, None, ALU.mult, reads=[r_t3], writes=[r_t3])
                    yield
                    P.ts(GV[:], GV[:], SS[:, 8:9], None, ALU.subtract, reads=[r_GV, r_t3], writes=[r_GV])
                    yield
                    P.tt(GV2[:], GV[:], GV[:], ALU.mult, reads=[r_GV], writes=[r_VN])
                    yield
                    P.op("dve", lambda e: e.tensor_reduce(SS[:, 9:10], GV2[:], AX.X, ALU.add), reads=[r_VN], writes=[r_t3])
                    yield
                    P.act(SS[:, 9:10], SS[:, 9:10], AF.Sqrt, reads=[r_t3, self.r_const], writes=[r_t3], bias=self.EPSC[:, 0:1], scale=1.0 / 512)
                    yield
                    P.recip(SS[:, 9:10], SS[:, 9:10], reads=[r_t3], writes=[r_t3])
                    yield
                    P.stt(GV2[:], GV[:], SS[:, 9:10], LNG[:], ALU.mult, ALU.mult, reads=[r_GV, r_t3, r_c], writes=[r_VN])
                    yield
                    P.tt(VN[:], GV2[:], LNB[:], ALU.add, reads=[r_VN, r_c], writes=[r_VN])
                    py = self.bank(7)
                    for g in range(4):
                        P.mm(py[:, g * 128:(g + 1) * 128], VN[:, g * 128:(g + 1) * 128], WST[:, g, :], True, True,
                             reads=[r_VN, r_c], writes=[self.r_ps[7]])
                    yield
                    P.tt(TMPY[:], py[:, :], BSB[:].rearrange("p g i -> p (g i)"), ALU.add, reads=[self.r_ps[7], r_c], writes=[r_TMPY])
                    yield
                    P.tt(YAs[:, :, tb * 128:(tb + 1) * 128], TMPY[:].rearrange("p (g i) -> p g i", g=4),
                         AU[:, :, tb * 128:(tb + 1) * 128], ALU.mult, reads=[r_TMPY, r_AU], writes=[r_YAs])

                for tb in range(ntb):
                    gens = [chain_av(tb), chain_l2(tb), chain_ab(tb)]
                    while gens:
                        for g_ in list(gens):
                            try:
                                next(g_)
                            except StopIteration:
                                gens.remove(g_)
                    P.dma(self.QKV[tokg + tb * 128:tokg + (tb + 1) * 128, :], ST[tb][:], reads=[r_ST[tb], r_STg[tb]],
                          writes=[self.r_QKV])
                P.dma(self.YA[:, :, tokg:tokg + n_out], YAs[:, :, :n_out], reads=[r_YAs], writes=[self.r_YA])
            P.barrier()

    def gdn_scan(self, bi, O, r_O):
        P = self.P
        with ExitStack() as es:
            def T(name, shape, dt=F32):
                return P.sbuf(name, shape, dt, es)

            def T2(name, shape, dt=F32):
                return [P.sbuf(f"{name}{i}", shape, dt, es) for i in range(2)]

            def R2():
                return [P.res(), P.res()]
            IDN = T("IDNg", [128, 128])
            MC = T("MC", [64, 8, 64])
            MS = T("MS", [64, 8, 64])
            MSi = T("MSi", [64, 8, 64])
            MN = T("MN", [64, 8, 64])
            MNi = T("MNi", [64, 8, 64])
            M2 = T("M2", [64, 2, 64])
            S2 = T("S2", [64, 2, 64])
            ON = T("ONf", [64, 128])
            r_c = P.res()
            P.dma(IDN[:], self.ident, writes=[r_c])
            P.dma(M2[:], self.maskc, writes=[r_c])
            P.dma(S2[:], self.strict, writes=[r_c])
            P.memset(ON[:], 1.0, writes=[r_c])
            for d in range(2):
                for (dst, src, dd) in ((MC, M2, d), (MS, S2, d), (MNi, M2, 1 - d), (MSi, S2, 1 - d)):
                    P.copy(dst[:, 4 * d:4 * d + 4, :], self.ap(src, dd * 64, [64, [0, 4], [1, 64]]), reads=[r_c], writes=[r_c])
            P.ts(MN[:], MC[:], -1.0, 30000.0, ALU.add, ALU.mult, reads=[r_c], writes=[r_c])
            P.ts(MNi[:], MNi[:], -1.0, 30000.0, ALU.add, ALU.mult, reads=[r_c], writes=[r_c])
            IDB = T("IDB", [64, 64], BF16)
            P.copy(IDB[:], IDN[0:64, 0:64], reads=[r_c], writes=[r_c])
            X = T2("X", [64, 2, 1552])
            r_X = R2()
            SM = T2("SM", [64, 6, 8])
            EG = T2("EG", [128, 8])
            r_s = R2()
            DEC, DECi = T("DEC", [64, 8, 64]), T("DECi", [64, 8, 64])
            Gm = DECi
            r_DEC = P.res()
            r_Gm = r_DEC
            KB, QG = T("KB", [64, 8, 128], BF16), T("QG", [64, 8, 128], BF16)
            r_kq = P.res()
            KBG, VB, KDEC = T2("KBG", [64, 8, 128], BF16), T2("VB", [64, 8, 128], BF16), T2("KDEC", [64, 8, 128], BF16)
            r_tm = R2()
            X16 = T("X16", [64, 2, 1024], BF16)
            r_X16 = P.res()
            KT, KBT, QT = [T(n, [128, 8, 64], BF16) for n in ("KT", "KBT", "QT")]
            r_fm = P.res()
            QGT = T2("QGT", [128, 8, 64], BF16)
            r_QGT = R2()
            X0, Y0 = T2("X0", [64, 8, 64]), T2("Y0", [64, 8, 64])
            r_X0, r_Y0 = R2(), R2()
            ATT = T2("ATT", [64, 8, 64], BF16)
            r_ATT = R2()
            Xa, Xb, Ya, Yb, Pm = [T(n, [64, 8, 64]) for n in ("Xa", "Xb", "Ya", "Yb", "Pm")]
            r_Xa, r_Xb, r_Ya, r_Yb, r_Pm = [P.res() for _ in range(5)]
            Pm16 = T("Pm16", [64, 8, 64], BF16)
            r_Pm16 = P.res()
            NWT = T("NWT", [128, 8, 64], BF16)
            r_NWT = P.res()
            VNEW = T("VNEW", [64, 8, 128], BF16)
            r_VNEW = P.res()
            S = T("S", [128, 8, 128])
            S16 = T("S16", [128, 8, 128], BF16)
            r_S, r_S16 = P.res(), P.res()
            P.memset(S[:], 0.0, writes=[r_S])
            P.memset(S16[:], 0.0, writes=[r_S16])
            P.memset(O[:], 0.0, writes=[r_O])
            bk = self.bank
            rp = self.r_ps
            NS = 36

            def b3(ps):
                return ps.rearrange("p (u i) -> p u i", u=8)

            def chunks(n):
                return n, ((3 - n) if n < 4 else (39 - n))

            def load(n):
                if n >= NS:
                    return
                cf, cb = chunks(n)
                x, r_x = X[n % 2], r_X[n % 2]
                P.dma(x[:, 0, :], self.QKV[cf * 64:(cf + 1) * 64, :], reads=[self.r_QKV], writes=[r_x])
                P.dma(x[:, 1, :], self.QKV[cb * 64:(cb + 1) * 64, :], reads=[self.r_QKV], writes=[r_x])

            def prologue(n):
                p = n % 2
                x, r_x = X[p], r_X[p]
                sm, rs = SM[p], r_s[p]
                G8, B8, GC, E1, E2, BE1 = [sm[:, i, :] for i in range(6)]
                eg = EG[p]
                load(n + 1)

                def xv(kind):
                    return self.ap(x, kind * 512, [64, [1552, 2], [128, 4], [1, 128]])

                def xu(kind, u):
                    return self.ap(X16, (u // 4) * 1024 + kind * 512 + (u % 4) * 128, [64, [1, 128]])

                def v4(t):
                    return t[:].rearrange("p (d h) k -> p d h k", d=2)

                def smb(i, inner):
                    return self.ap(sm, i * 8, [64, [1, 8], [0, inner]])

                def smb4(i):
                    return self.ap(sm, i * 8, [64, [4, 2], [1, 4], [0, 128]])
                P.copy(X16[:], x[:, :, 0:1024], reads=[r_x], writes=[r_X16], eng="pool")
                P.copy(self.ap(sm, 0, [64, [4, 2], [1, 4]]), self.ap(x, 1536, [64, [1552 + 4, 2], [1, 4]]), reads=[r_x], writes=[rs])
                P.copy(self.ap(sm, 8, [64, [4, 2], [1, 4]]), self.ap(x, 1544, [64, [1552 + 4, 2], [1, 4]]), reads=[r_x], writes=[rs])
                p0 = bk(0)
                for d in range(2):
                    P.mm(p0[0:64, 4 * d:4 * d + 4], M2[:, d, :], G8[:, 4 * d:4 * d + 4], True, True, reads=[r_c, rs], writes=[rp[0]])
                P.mm(p0[:, 8:16], ON[:], G8, True, True, reads=[r_c, rs], writes=[rp[0]])
                yield
                P.copy(GC, p0[0:64, 0:8], reads=[rp[0]], writes=[rs])
                P.act(E1, p0[0:64, 0:8], AF.Exp, reads=[rp[0]], writes=[rs])
                P.tt(E2, p0[0:64, 8:16], GC, ALU.subtract, reads=[rp[0], rs], writes=[rs])
                P.act(E2, E2, AF.Exp, reads=[rs], writes=[rs])
                P.act(eg[:], p0[:, 8:16], AF.Exp, reads=[rp[0]], writes=[rs])
                P.tt(BE1, B8, E1, ALU.mult, reads=[rs], writes=[rs])
                P.tt(Gm[:], MC[:], smb(0, 64), ALU.mult, reads=[r_c, rs], writes=[r_Gm])
                p1 = bk(4)
                P.mm(p1[0:64, :], ON[:, 0:64], Gm[:].rearrange("p u i -> p (u i)"), True, True, reads=[r_c, r_Gm], writes=[rp[4]])
                yield
                gcb = smb(2, 64)
                P.tt(DEC[:], b3(p1[0:64, :]), MN[:], ALU.add, reads=[rp[4], r_c], writes=[r_DEC])
                P.tt(DEC[:], DEC[:], gcb, ALU.subtract, reads=[r_DEC, rs], writes=[r_DEC])
                P.act(DEC[:], DEC[:], AF.Exp, reads=[r_DEC], writes=[r_DEC])
                yield
                P.tt(DECi[:], gcb, b3(p1[0:64, :]), ALU.subtract, reads=[rp[4], rs], writes=[r_DEC])
                P.tt(DECi[:], DECi[:], MNi[:], ALU.add, reads=[r_DEC, r_c], writes=[r_DEC])
                P.act(DECi[:], DECi[:], AF.Exp, reads=[r_DEC], writes=[r_DEC])
                yield
                P.tt(v4(KB), xv(1), smb4(1), ALU.mult, reads=[r_x, rs], writes=[r_kq])
                P.tt(v4(QG), xv(0), smb4(3), ALU.mult, reads=[r_x, rs], writes=[r_kq])
                yield
                P.tt(v4(KBG[p]), xv(1), smb4(5), ALU.mult, reads=[r_x, rs], writes=[r_tm[p]])
                P.tt(v4(VB[p]), xv(2), smb4(1), ALU.mult, reads=[r_x, rs], writes=[r_tm[p]])
                P.tt(v4(KDEC[p]), xv(1), smb4(4), ALU.mult, reads=[r_x, rs], writes=[r_tm[p]])
                yield
                for bi_, (src, dst, r_dst, pb) in enumerate(((1, KT, r_fm, 5), (KB, KBT, r_fm, 6), (0, QT, r_fm, 7),
                                                             (QG, QGT[p], r_QGT[p], 4))):
                    ps = bk(pb)
                    for u in range(8):
                        sap = xu(src, u) if isinstance(src, int) else src[:, u, :]
                        P.mm(ps[:, u * 64:(u + 1) * 64], sap, IDB[:], True, True,
                             reads=[r_X16, r_kq, r_c], writes=[rp[pb]])
                    P.copy(dst[:], b3(ps[:, :]), reads=[rp[pb]], writes=[r_dst], eng="act" if bi_ % 2 else "dve")
                    yield
                pA, pAi, pAt = bk(5), bk(6), bk(7)
                for u in range(8):
                    P.mm(pA[0:64, u * 64:(u + 1) * 64], KT[:, u, :], KBT[:, u, :], True, True, reads=[r_fm], writes=[rp[5]])
                for u in range(8):
                    P.mm(pAi[0:64, u * 64:(u + 1) * 64], KBT[:, u, :], KT[:, u, :], True, True, reads=[r_fm], writes=[rp[6]])
                for u in range(8):
                    P.mm(pAt[0:64, u * 64:(u + 1) * 64], KT[:, u, :], QT[:, u, :], True, True, reads=[r_fm], writes=[rp[7]])
                yield
                x0, y0 = X0[p], Y0[p]
                P.stt(x0[:], b3(pA[0:64, :]), -1.0, DEC[:], ALU.mult, ALU.mult, reads=[rp[5], r_DEC], writes=[r_X0[p]])
                P.tt(x0[:], x0[:], MS[:], ALU.mult, reads=[r_X0[p], r_c], writes=[r_X0[p]])
                yield
                P.stt(y0[:], b3(pAi[0:64, :]), -1.0, DECi[:], ALU.mult, ALU.mult, reads=[rp[6], r_DEC], writes=[r_Y0[p]])
                P.tt(y0[:], y0[:], MSi[:], ALU.mult, reads=[r_Y0[p], r_c], writes=[r_Y0[p]])
                P.tt(ATT[p][:], b3(pAt[0:64, :]), DEC[:], ALU.mult, reads=[rp[7], r_DEC], writes=[r_ATT[p]])
                yield

            def inverse(n, nxt):
                p = n % 2
                P.tt(Pm[:], X0[p][:], self.ap(IDN, 0, [64, [0, 8], [1, 64]]), ALU.add, reads=[r_X0[p], r_c], writes=[r_Pm])
                Xc, r_Xc, Yc, r_Yc = X0[p], r_X0[p], Y0[p], r_Y0[p]
                tgt = [(Xa, r_Xa, Ya, r_Ya), (Xb, r_Xb, Yb, r_Yb)]
                for lev in range(1, 6):
                    Xn, r_Xn, Yn, r_Yn = tgt[lev % 2]
                    py, px, pp = bk(1), bk(2), bk(3)
                    for u in range(8):
                        P.mm(py[0:64, u * 64:(u + 1) * 64], Xc[:, u, :], Yc[:, u, :], True, True, reads=[r_Xc, r_Yc], writes=[rp[1]])
                    P.copy(Yn[:], b3(py[0:64, :]), reads=[rp[1]], writes=[r_Yn], eng="act")
                    if lev < 5:
                        for u in range(8):
                            P.mm(px[0:64, u * 64:(u + 1) * 64], Yc[:, u, :], Xc[:, u, :], True, True, reads=[r_Xc, r_Yc], writes=[rp[2]])
                        P.copy(Xn[:], b3(px[0:64, :]), reads=[rp[2]], writes=[r_Xn], eng="act")
                    for u in range(8):
                        P.mm(pp[0:64, u * 64:(u + 1) * 64], Yn[:, u, :], Pm[:, u, :], True, True, reads=[r_Yn, r_Pm], writes=[rp[3]])
                    P.tt(Pm[:], Pm[:], b3(pp[0:64, :]), ALU.add, reads=[rp[3], r_Pm], writes=[r_Pm])
                    Xc, r_Xc, Yc, r_Yc = Xn, r_Xn, Yn, r_Yn
                    for _ in range(3):
                        next(nxt, None)

            def tail(n):
                p = n % 2
                cf, cb = chunks(n)
                P.copy(Pm16[:], Pm[:], reads=[r_Pm], writes=[r_Pm16], eng="act")
                pw = bk(4)
                for u in range(8):
                    P.mm(pw[:, u * 64:(u + 1) * 64], KBG[p][:, u, :], Pm16[:, u, :], True, True, reads=[r_tm[p], r_Pm16], writes=[rp[4]])
                P.ts(NWT[:], b3(pw[:, :]), -1.0, None, ALU.mult, reads=[rp[4]], writes=[r_NWT])
                for u in range(8):
                    pb = 5 + u // 4
                    pv = bk(pb)
                    c0 = (u % 4) * 128
                    P.mm(pv[0:64, c0:c0 + 128], Pm16[:, u, :], VB[p][:, u, :], True, False, reads=[r_Pm16, r_tm[p]], writes=[rp[pb]])
                    P.mm(pv[0:64, c0:c0 + 128], NWT[:, u, :], S16[:, u, :], False, True, reads=[r_NWT, r_S16], writes=[rp[pb]])
                for hv in range(2):
                    P.copy(VNEW[:, 4 * hv:4 * hv + 4, :], bk(5 + hv)[0:64, :].rearrange("p (u d) -> p u d", u=4),
                           reads=[rp[5 + hv]], writes=[r_VNEW], eng="act" if hv else "dve")
                po = bk(7)
                for u in range(8):
                    P.mm(po[:, u * 64:(u + 1) * 64], S16[:, u, :], QGT[p][:, u, :], True, False, reads=[r_S16, r_QGT[p]], writes=[rp[7]])
                    P.mm(po[:, u * 64:(u + 1) * 64], VNEW[:, u, :], ATT[p][:, u, :], False, True, reads=[r_VNEW, r_ATT[p]], writes=[rp[7]])
                for d, ck in ((0, cf), (1, cb)):
                    P.tt(O[:, :, ck * 64:(ck + 1) * 64], O[:, :, ck * 64:(ck + 1) * 64],
                         po[:, 256 * d:256 * (d + 1)].rearrange("p (h i) -> p h i", h=4), ALU.add,
                         reads=[rp[7], r_O], writes=[r_O])
                for hv in range(2):
                    pb = 1 + hv
                    psu = bk(pb)
                    for uu in range(4):
                        u = 4 * hv + uu
                        P.mm(psu[:, uu * 128:(uu + 1) * 128], KDEC[p][:, u, :], VNEW[:, u, :], True, True,
                             reads=[r_tm[p], r_VNEW], writes=[rp[pb]])
                P.tt(S[:], S[:], self.ap(EG[p], 0, [128, [1, 8], [0, 128]]), ALU.mult, reads=[r_S, r_s[p]], writes=[r_S])
                for hv in range(2):
                    P.tt(S[:, 4 * hv:4 * hv + 4, :], S[:, 4 * hv:4 * hv + 4, :],
                         bk(1 + hv)[:, :].rearrange("p (u d) -> p u d", u=4), ALU.add, reads=[rp[1 + hv], r_S], writes=[r_S])
                P.copy(S16[:], S[:], reads=[r_S], writes=[r_S16], eng="act")

            load(0)
            for _ in prologue(0):
                pass
            for n in range(NS):
                nxt = prologue(n + 1) if n + 1 < NS else iter(())
                inverse(n, nxt)
                for _ in nxt:
                    pass
                tail(n)
            P.barrier()

    def ab_out(self, bi, O, r_O):
        P = self.P
        l = 0
        with ExitStack() as es:
            WO = P.sbuf("WOa", [128, 8, D], BF16, es)
            r_WO = P.res()
            P.dma(WO[:], self.WOA, reads=[self.r_WA], writes=[r_WO])
            NG = P.sbuf("NGb", [128, 1], F32, es)
            r_NG = P.res()
            P.dma(NG[:], self.b_ng, writes=[r_NG])
            YAt = [P.sbuf(f"YAt{i}", [128, 4, 512], BF16, es) for i in range(2)]
            GTt = [P.sbuf(f"GTt{i}", [128, 4, 512], BF16, es) for i in range(2)]
            r_YAt, r_GTt = [P.res(), P.res()], [P.res(), P.res()]
            SQ = P.sbuf("SQo", [128, 4, 512], BF16, es)
            r_SQ = P.res()
            RS = P.sbuf("RSo", [128, 512], F32, es)
            r_RS = P.res()
            YB = P.sbuf("YB", [128, 4, 512], BF16, es)
            TMP = P.sbuf("TMPo", [128, 512], F32, es)
            r_YB, r_TMP = P.res(), P.res()
            tiles = [("c", 0, CTX)] + [("l", t * 512, (t + 1) * 512) for t in range(4)]
            for ti, (which, a, b) in enumerate(tiles):
                n = b - a
                R, r_R, jj, g0 = (self.Rl, self.r_Rl, bi, CTX) if which == "l" else (self.Rc, self.r_Rc, self.nb, 0)
                ga = g0 + a
                ya, gt = YAt[ti % 2], GTt[ti % 2]
                P.dma(ya[:, :, :n], self.YA[:, :, ga:ga + n], reads=[self.r_YA], writes=[r_YAt[ti % 2]])
                P.dma(gt[:, :, :n], self.GATE[:, :, ga:ga + n], reads=[self.r_GATE], writes=[r_GTt[ti % 2]])
                for h in range(4):
                    pb = h % 2
                    ps = self.bank(pb)
                    P.act(SQ[:, h, :n], O[:, h, ga:ga + n], AF.Square, reads=[r_O], writes=[r_SQ])
                    P.mm(ps[:, :n], self.ONESb[:], SQ[:, h, :n], True, True, reads=[r_SQ, self.r_const], writes=[self.r_ps[pb]])
                    P.act(RS[:, :n], ps[:, :n], AF.Sqrt, reads=[self.r_ps[pb], self.r_const], writes=[r_RS], scale=1.0 / 128, bias=self.EPSC[:, 0:1])
                    P.recip(RS[:, :n], RS[:, :n], reads=[r_RS], writes=[r_RS])
                    P.stt(TMP[:, :n], O[:, h, ga:ga + n], NG[:, 0:1], RS[:, :n], ALU.mult, ALU.mult, reads=[r_O, r_NG, r_RS], writes=[r_TMP])
                    P.tt(YB[:, h, :n], TMP[:, :n], gt[:, h, :n], ALU.mult, reads=[r_TMP, r_GTt[ti % 2]], writes=[r_YB])
                for o in range(8):
                    pb = 2 + o % 3
                    ps = self.bank(pb)
                    for kc in range(8):
                        src = ya[:, kc, :n] if kc < 4 else YB[:, kc - 4, :n]
                        P.mm(ps[:, :n], WO[:, kc, o * 128:(o + 1) * 128], src, kc == 0, kc == 7,
                             reads=[r_WO, r_YAt[ti % 2], r_YB], writes=[self.r_ps[pb]])
                    P.stt(R[:, o, a:b], ps[:, :n], self.MOD[:, l, 2 * 8 + o, jj:jj + 1], R[:, o, a:b], ALU.mult, ALU.add,
                          reads=[self.r_ps[pb], self.r_MOD, r_R], writes=[r_R])
            P.barrier()

    def ab(self, bi):
        P = self.P
        self.ab_proj(bi)
        with ExitStack() as es:
            O = P.sbuf("Og", [128, 4, CTX + SEQ], F32, es)
            r_O = P.res()
            self.gdn_scan(bi, O, r_O)
            self.ab_out(bi, O, r_O)

    def mla(self, bi):
        P = self.P
        j = bi
        jc = self.nb
        l = 1
        NK = CTX + SEQ
        scale = (128 + 64) ** -0.5
        with ExitStack() as es0:
            CQN = P.sbuf("CQN", [128, 3, SEQ], BF16, es0)
            CKVN = P.sbuf("CKVN", [128, 2, NK], BF16, es0)
            KR = P.sbuf("KR", [64, NK], BF16, es0)
            COS = P.sbuf("COS", [64, SEQ], F32, es0)
            SIN = P.sbuf("SIN", [64, SEQ], F32, es0)
            r_CQN, r_CKVN, r_KR, r_rope = P.res(), P.res(), P.res(), P.res()
            P.dma(COS[:], self.ropeC, writes=[r_rope])
            P.dma(SIN[:], self.ropeS, writes=[r_rope])
            with ExitStack() as es:
                self.norm_alloc(es)
                WIN = P.sbuf("WIN", [128, 8, 768], BF16, es)
                r_WIN = P.res()
                P.dma(WIN[:], self.WIN1, reads=[self.r_W1], writes=[r_WIN])
                NG = P.sbuf("NG", [128, 5], F32, es)
                r_NG = P.res()
                P.dma(NG[:, 0:3], self.m_qng, writes=[r_NG])
                P.dma(NG[:, 3:5], self.m_kvng, writes=[r_NG])
                HT = P.sbuf("HTm", [128, 8, 512], BF16, es)
                r_HT = P.res()
                SQ2 = P.sbuf("SQ2", [128, 3, 512], BF16, es)
                r_SQ2 = P.res()
                RS2 = P.sbuf("RS2", [128, 512], F32, es)
                r_RS2 = P.res()
                T1 = P.sbuf("T1", [64, 512], F32, es)
                T2 = P.sbuf("T2", [64, 512], F32, es)
                r_T = P.res()
                P.memset(self.EPSC[:, 1:2], 384.0 * EPS, writes=[self.r_const])
                P.memset(self.EPSC[:, 2:3], 256.0 * EPS, writes=[self.r_const])
                tiles = [("c", 0, CTX)] + [("l", t * 512, (t + 1) * 512) for t in range(4)]
                for which, a, b in tiles:
                    n = b - a
                    R, r_R, jj = (self.Rl, self.r_Rl, j) if which == "l" else (self.Rc, self.r_Rc, jc)
                    k0 = a + CTX if which == "l" else a
                    self.norm_tile(R, r_R, a, b, HT, r_HT, 0, l, 0, jj, 7)
                    groups = []
                    if which == "l":
                        groups.append((0, 3, CQN, r_CQN, a, 0, 1))
                    groups.append((3, 2, CKVN, r_CKVN, k0, 3, 2))
                    for c0, ncnk, DST, r_DST, d0, ng0, epsi in groups:
                        for ci in range(ncnk):
                            ps = self.bank(ci)
                            for kc in range(8):
                                P.mm(ps[:, :n], WIN[:, kc, (c0 + ci) * 128:(c0 + ci + 1) * 128], HT[:, kc, :n],
                                     kc == 0, kc == 7, reads=[r_WIN, r_HT], writes=[self.r_ps[ci]])
                            P.act(SQ2[:, ci, :n], ps[:, :n], AF.Square, reads=[self.r_ps[ci]], writes=[r_SQ2])
                        pss = self.bank(3)
                        for ci in range(ncnk):
                            P.mm(pss[:, :n], self.ONESb[:], SQ2[:, ci, :n], ci == 0, ci == ncnk - 1,
                                 reads=[r_SQ2, self.r_const], writes=[self.r_ps[3]])
                        P.act(RS2[:, :n], pss[:, :n], AF.Sqrt, reads=[self.r_ps[3], self.r_const], writes=[r_RS2],
                              scale=1.0 / (ncnk * 128), bias=self.EPSC[:, 0:1])
                        P.recip(RS2[:, :n], RS2[:, :n], reads=[r_RS2], writes=[r_RS2])
                        for ci in range(ncnk):
                            ps = self.bank(ci)
                            P.stt(DST[:, ci, d0:d0 + n], ps[:, :n], NG[:, ng0 + ci:ng0 + ci + 1], RS2[:, :n],
                                  ALU.mult, ALU.mult, reads=[self.r_ps[ci], r_NG, r_RS2], writes=[r_DST])
                    pk = self.bank(4)
                    pks = self.bank(5)
                    for kc in range(8):
                        P.mm(pk[0:64, :n], WIN[:, kc, 640:704], HT[:, kc, :n], kc == 0, kc == 7,
                             reads=[r_WIN, r_HT], writes=[self.r_ps[4]])
                    if which == "l":
                        for kc in range(8):
                            P.mm(pks[0:64, :n], WIN[:, kc, 704:768], HT[:, kc, :n], kc == 0, kc == 7,
                                 reads=[r_WIN, r_HT], writes=[self.r_ps[5]])
                        P.tt(T1[:, :n], pk[0:64, :n], COS[:, a:b], ALU.mult, reads=[self.r_ps[4], r_rope], writes=[r_T])
                        P.tt(T2[:, :n], pks[0:64, :n], SIN[:, a:b], ALU.mult, reads=[self.r_ps[5], r_rope, r_T], writes=[r_T])
                        P.tt(KR[:, k0:k0 + n], T1[:, :n], T2[:, :n], ALU.add, reads=[r_T], writes=[r_KR])
                    else:
                        P.copy(KR[:, k0:k0 + n], pk[0:64, :n], reads=[self.r_ps[4]], writes=[r_KR])
                P.barrier()
            with ExitStack() as es:
                WUQs = P.sbuf("WUQs", [128, 3, 2048], BF16, es)
                WUKVs = P.sbuf("WUKVs", [128, 2, 2048], BF16, es)
                WOs = P.sbuf("WOs", [128, 4, D], BF16, es)
                r_WOs = P.res()
                r_Ws = P.res()
                P.dma(WUQs[:], self.WUQ, reads=[self.r_W1], writes=[r_Ws])
                P.dma(WUKVs[:], self.WUKV, reads=[self.r_W1], writes=[r_Ws])
                KN = P.sbuf("KN", [128, 4, NK], BF16, es)
                V = P.sbuf("V", [128, 18, 512], BF16, es)
                r_KN, r_V = P.res(), P.res()
                QN = P.sbuf("QN", [128, 4, 512], BF16, es)
                QR = P.sbuf("QR", [64, 4, 512], BF16, es)
                r_QN, r_QR = P.res(), P.res()
                T1 = P.sbuf("T1b", [64, 512], F32, es)
                T2 = P.sbuf("T2b", [64, 512], F32, es)
                r_T = P.res()
                PT = [P.sbuf(f"PT{i}", [128, 512], BF16, es) for i in range(3)]
                r_PT = [P.res() for _ in range(3)]
                RSM = P.sbuf("RSM", [128, 512], F32, es)
                r_RSM = P.res()
                OA = P.sbuf("OA", [128, 4, 512], BF16, es)
                r_OA = P.res()
                pti = 0
                for hg in range(2):
                    P.dma(WOs[:], self.WO1[:, 4 * hg:4 * hg + 4, :], reads=[self.r_W1], writes=[r_WOs])
                    ktiles = [(0, CTX)] + [(CTX + t * 512, CTX + (t + 1) * 512) for t in range(4)]
                    cnt = 0
                    for hh in range(4):
                        h = 4 * hg + hh
                        for (ka, kb) in ktiles:
                            n = kb - ka
                            pb = 5 + cnt % 3
                            cnt += 1
                            ps = self.bank(pb)
                            for c in range(2):
                                P.mm(ps[:, :n], WUKVs[:, c, h * 128:(h + 1) * 128], CKVN[:, c, ka:kb], c == 0, c == 1,
                                     reads=[r_Ws, r_CKVN], writes=[self.r_ps[pb]])
                            P.act(KN[:, hh, ka:kb], ps[:, :n], AF.Copy, reads=[self.r_ps[pb]], writes=[r_KN])
                    for kt in range(18):
                        pb = 5 + cnt % 3
                        cnt += 1
                        ps = self.bank(pb)
                        for c in range(2):
                            P.mm(ps[:, :], CKVN[:, c, kt * 128:(kt + 1) * 128],
                                 WUKVs[:, c, 1024 + 512 * hg:1024 + 512 * (hg + 1)], c == 0, c == 1,
                                 reads=[r_Ws, r_CKVN], writes=[self.r_ps[pb]])
                        P.copy(V[:, kt, :], ps[:, :], reads=[self.r_ps[pb]], writes=[r_V])
                    for qt in range(4):
                        qa, qb = qt * 512, (qt + 1) * 512
                        for hh in range(4):
                            h = 4 * hg + hh
                            pb = 5 + cnt % 3
                            cnt += 1
                            ps = self.bank(pb)
                            for c in range(3):
                                P.mm(ps[:, :], WUQs[:, c, h * 256:h * 256 + 128], CQN[:, c, qa:qb], c == 0, c == 2,
                                     reads=[r_Ws, r_CQN], writes=[self.r_ps[pb]])
                            P.act(QN[:, hh, :], ps[:, :], AF.Copy, reads=[self.r_ps[pb]], writes=[r_QN])
                            pb1 = 5 + cnt % 3
                            cnt += 1
                            pb2 = 5 + cnt % 3
                            cnt += 1
                            p1, p2 = self.bank(pb1), self.bank(pb2)
                            for c in range(3):
                                P.mm(p1[0:64, :], WUQs[:, c, h * 256 + 128:h * 256 + 192], CQN[:, c, qa:qb], c == 0, c == 2,
                                     reads=[r_Ws, r_CQN], writes=[self.r_ps[pb1]])
                            for c in range(3):
                                P.mm(p2[0:64, :], WUQs[:, c, h * 256 + 192:h * 256 + 256], CQN[:, c, qa:qb], c == 0, c == 2,
                                     reads=[r_Ws, r_CQN], writes=[self.r_ps[pb2]])
                            P.tt(T1[:, :], p1[0:64, :], COS[:, qa:qb], ALU.mult, reads=[self.r_ps[pb1], r_rope], writes=[r_T])
                            P.tt(T2[:, :], p2[0:64, :], SIN[:, qa:qb], ALU.mult, reads=[self.r_ps[pb2], r_rope, r_T], writes=[r_T])
                            P.tt(QR[:, hh, :], T1[:, :], T2[:, :], ALU.add, reads=[r_T], writes=[r_QR])
                        for hh in range(4):
                            bo, bs_ = (3, 4) if hh % 2 == 0 else (5, 6)
                            po, psm = self.bank(bo), self.bank(bs_)
                            def s_stage(kt, sb):
                                pss = self.bank(sb)
                                P.mm(pss[:, :], KN[:, hh, kt * 128:(kt + 1) * 128], QN[:, hh, :], True, False,
                                     reads=[r_KN, r_QN], writes=[self.r_ps[sb]])
                                P.mm(pss[:, :], KR[:, kt * 128:(kt + 1) * 128], QR[:, hh, :], False, True,
                                     reads=[r_KR, r_QR], writes=[self.r_ps[sb]])
                                P.act(PT[sb][:], pss[:, :], AF.Exp, reads=[self.r_ps[sb]], writes=[r_PT[sb]], scale=scale)

                            def pv_stage(kt, sb):
                                P.mm(po[:, :], V[:, kt, hh * 128:(hh + 1) * 128], PT[sb][:], kt == 0, kt == 17,
                                     reads=[r_V, r_PT[sb]], writes=[self.r_ps[bo]])
                                P.mm(psm[:, :], self.ONESb[:], PT[sb][:], kt == 0, kt == 17,
                                     reads=[r_PT[sb], self.r_const], writes=[self.r_ps[bs_]])
                            SK = 2
                            sbs = []
                            for kt in range(18 + SK):
                                if kt < 18:
                                    sbs.append(pti % 3)
                                    pti += 1
                                    s_stage(kt, sbs[kt])
                                if kt >= SK:
                                    pv_stage(kt - SK, sbs[kt - SK])
                            P.recip(RSM[:], psm[:, :], reads=[self.r_ps[bs_]], writes=[r_RSM])
                            P.tt(OA[:, hh, :], po[:, :], RSM[:], ALU.mult, reads=[self.r_ps[bo], r_RSM], writes=[r_OA])
                        for o in range(8):
                            pb = 5 + cnt % 3
                            cnt += 1
                            ps = self.bank(pb)
                            for hh in range(4):
                                P.mm(ps[:, :], WOs[:, hh, o * 128:(o + 1) * 128], OA[:, hh, :], hh == 0, hh == 3,
                                     reads=[r_WOs, r_OA], writes=[self.r_ps[pb]])
                            P.stt(self.Rl[:, o, qa:qb], ps[:, :], self.MOD[:, l, 2 * 8 + o, j:j + 1], self.Rl[:, o, qa:qb],
                                  ALU.mult, ALU.add, reads=[self.r_ps[pb], self.r_MOD, self.r_Rl], writes=[self.r_Rl])
                P.barrier()

    def final(self, bi):
        P = self.P
        R, r_R = self.Rl, self.r_Rl
        with ExitStack() as es:
            self.norm_alloc(es)
            FG = P.sbuf("FG", [128, 8], F32, es)
            r_FG = P.res()
            P.dma(FG[:], self.fing, writes=[r_FG])
            OT = [P.sbuf(f"OT{i}", [128, 8, 512], F32, es) for i in range(2)]
            r_OT = [P.res(), P.res()]
            for ti in range(4):
                a, b = ti * 512, (ti + 1) * 512
                n = 512
                i = ti % 2
                SQ, RS = self.SQ, self.RS[i]
                P.act(SQ[:, :, :n], R[:, :, a:b], AF.Square, reads=[r_R], writes=[self.r_SQ])
                psb = 6 + i
                ps = self.bank(psb)
                for kc in range(8):
                    P.mm(ps[:, :n], self.ONESb[:], SQ[:, kc, :n], kc == 0, kc == 7,
                         reads=[self.r_SQ, self.r_const], writes=[self.r_ps[psb]])
                P.act(RS[:, :n], ps[:, :n], AF.Sqrt, reads=[self.r_ps[psb]], writes=[self.r_RS[i]],
                      scale=1.0 / D, bias=self.eps_ap(EPS))
                P.recip(RS[:, :n], RS[:, :n], reads=[self.r_RS[i]], writes=[self.r_RS[i]])
                for kc in range(8):
                    P.stt(OT[i][:, kc, :], R[:, kc, a:b], FG[:, kc:kc + 1], RS[:, :n], ALU.mult, ALU.mult,
                          reads=[r_R, self.r_RS[i], r_FG], writes=[r_OT[i]])
                d = P.dma(self.outT[bi][:, :, a:b], OT[i][:], reads=[r_OT[i]])
                self.out_dmas.append(d)
            P.barrier()

    def build(self):
        P = self.P
        st = self.stages
        self.cast_weights()
        self.phase_mod()
        for bi in range(self.nb):
            P.dma(self.Rl[:], self.xT[bi], writes=[self.r_Rl])
            P.dma(self.Rc[:], self.cT[bi], writes=[self.r_Rc])
            if "ab" in st:
                self.ab(bi)
            if "ffn0" in st:
                self.ffn(0, bi, "l")
                self.ffn(0, self.nb, "c")
            if "mla" in st:
                self.mla(bi)
            if "ffn1" in st:
                self.ffn(1, bi, "l")
            self.final(bi) if "final" in st else self.dump(bi)
        P.emit(self.out_dmas)
        return self.nc

    def dump(self, bi):
        P = self.P
        for kc in range(8):
            d = P.dma(self.outT[bi][:, kc, :], self.Rl[:, kc, :], reads=[self.r_Rl])
            self.out_dmas.append(d)
        P.barrier()


def fm(v, n=None):
    v = np.asarray(v, np.float32)
    lead = v.shape[:-1]
    k = v.shape[-1] // 128
    v = v.reshape(*lead, k, 128)
    return np.ascontiguousarray(np.moveaxis(v, -1, 0))


def prep_shared(I):
    S = {}
    S["ada_w"] = np.ascontiguousarray(I["ada_w"], np.float32)
    S["ada_bT"] = fm(I["ada_b"])
    S["n1g"] = fm(I["norm1_g"])
    S["n2g"] = fm(I["norm2_g"])
    S["fing"] = fm(I["final_g"])
    S["ffn_w_up"] = np.ascontiguousarray(I["ffn_w_up"], np.float32)
    S["ffn_w_down"] = np.ascontiguousarray(I["ffn_w_down"], np.float32)
    cw = np.asarray(I["ffn_conv_w"], np.float32)
    S["ffn_cw"] = np.ascontiguousarray(cw.reshape(2, 3, 44, 128).transpose(3, 0, 2, 1))
    S["a_win"] = np.ascontiguousarray(I["ab_w_in"][0], np.float32)
    S["a_wout"] = np.ascontiguousarray(I["ab_w_out"][0], np.float32)
    S["a_lng"] = np.ascontiguousarray(np.broadcast_to(np.asarray(I["a_ln_g"][0], np.float32)[None], (128, 512)))
    S["a_lnb"] = np.ascontiguousarray(np.broadcast_to(np.asarray(I["a_ln_b"][0], np.float32)[None], (128, 512)))
    S["a_wsT"] = np.ascontiguousarray(np.asarray(I["a_ws"][0], np.float32).transpose(2, 0, 1))
    S["a_bsb"] = np.ascontiguousarray(np.broadcast_to(np.asarray(I["a_bs"][0], np.float32)[None], (128, 4, 128)))
    bcw = np.asarray(I["b_conv_w"][0], np.float32)
    S["b_cw"] = np.ascontiguousarray(bcw.reshape(3, 12, 128).transpose(2, 1, 0))
    S["b_alog"] = np.ascontiguousarray(np.broadcast_to(np.asarray(I["b_a_log"][0], np.float32).reshape(1, 8), (128, 8)))
    S["b_dtb"] = np.ascontiguousarray(np.broadcast_to(np.asarray(I["b_dt_bias"][0], np.float32).reshape(1, 8), (128, 8)))
    S["b_ng"] = np.ascontiguousarray(np.asarray(I["b_norm_g"][0], np.float32).reshape(128, 1))
    S["ident"] = np.eye(128, dtype=np.float32)
    jj, ii = np.meshgrid(np.arange(64), np.arange(64), indexing="ij")
    S["maskc"] = np.ascontiguousarray(np.stack([(jj <= ii), (jj >= ii)], 1).astype(np.float32))
    S["strict"] = np.ascontiguousarray(np.stack([(jj < ii), (jj > ii)], 1).astype(np.float32))
    w_in = np.asarray(I["mla_w_in"][0], np.float32)
    perm = (np.arange(64) + 32) % 64
    S["m_win"] = np.ascontiguousarray(np.concatenate([w_in, w_in[:, 640 + perm]], 1))
    wuq = np.asarray(I["mla_w_uq"][0], np.float32).reshape(384, 8, 192)
    S["m_wuq"] = np.ascontiguousarray(np.concatenate([wuq, wuq[:, :, 128 + perm]], 2).reshape(384, 2048))
    wukv = np.asarray(I["mla_w_ukv"][0], np.float32).reshape(256, 8, 256)
    S["m_wukv"] = np.ascontiguousarray(np.concatenate([wukv[:, :, :128].reshape(256, 1024),
                                                       wukv[:, :, 128:].reshape(256, 1024)], 1))
    S["m_wout"] = np.ascontiguousarray(I["mla_w_out"][0], np.float32)
    S["m_qng"] = fm(I["mla_q_norm_g"][0])
    S["m_kvng"] = fm(I["mla_kv_norm_g"][0])
    pos = np.arange(SEQ)
    row = (pos // 64).astype(np.float32)
    col = (pos % 64).astype(np.float32)
    inv = (np.float32(10000.0) ** (-np.arange(16, dtype=np.float32) / np.float32(16))).astype(np.float32)
    ang = np.concatenate([row[:, None] * inv, col[:, None] * inv], -1).astype(np.float32)
    cs, sn = np.cos(ang).T.astype(np.float32), np.sin(ang).T.astype(np.float32)
    S["ropeC"] = np.ascontiguousarray(np.concatenate([cs, cs], 0))
    S["ropeS"] = np.ascontiguousarray(np.concatenate([-sn, sn], 0))
    return S


def prep_core(I, b0, nb):
    x = np.asarray(I["x"][b0:b0 + nb], np.float32)
    ctx = np.asarray(I["ctx"][b0:b0 + nb], np.float32)
    C = {}
    C["xT"] = np.ascontiguousarray(x.reshape(nb, SEQ, 8, 128).transpose(0, 3, 2, 1))
    C["cT"] = np.ascontiguousarray(ctx.reshape(nb, CTX, 8, 128).transpose(0, 3, 2, 1))
    cc = np.concatenate([np.asarray(I["c"][b0:b0 + nb], np.float32), np.asarray(I["c_ctx"], np.float32)[None]], 0)
    C["ccT"] = np.ascontiguousarray(cc.reshape(nb + 1, 8, 128).transpose(2, 1, 0))
    return C


_CACHE = {}


def run(I, nb, ncores, stages, trace=False):
    key = (nb, tuple(stages))
    if key not in _CACHE:
        _CACHE[key] = Builder(nb, stages)
        _CACHE[key].build()
    B = _CACHE[key]
    S = prep_shared(I)
    in_maps = []
    for c in range(ncores):
        m = dict(S)
        m.update(prep_core(I, c * nb, nb))
        in_maps.append({k: m[k] for k in B.inputs})
    res = run_bass_kernel_spmd(B.nc, in_maps, core_ids=list(range(ncores)), trace=trace)
    outs = []
    for r in res.results:
        o = r["outT"]
        outs.append(o.transpose(0, 3, 2, 1).reshape(nb, SEQ, D))
    return np.concatenate(outs, 0), res


ALL_STAGES = ("ab", "ffn0", "mla", "ffn1", "final")


def kernel(**inputs):
    out, _ = run(inputs, 4, NCORES, ALL_STAGES)
    return np.ascontiguousarray(out.astype(np.float32))
```
